# Optimizing a Trainium2 kernel written in Bass

```python
import math
import jax, jax.numpy as jnp
from jax import lax
import numpy as np

D_MODEL = 2048
BATCH = 2
SEQ = 8192
DEPTH = 2
DEC_BATCH = 16
DEC_SEQ = 64
PAST_LEN = 4096

CHUNK = 64
GROUP_WIDTH = D_MODEL // 2
MIX_WIDTH = 2 * GROUP_WIDTH
LRU_HEADS = 8
LRU_HEAD_DIM = GROUP_WIDTH // LRU_HEADS
LRU_C = 8.0
CONV_WIDTH = 4
MLP_CHUNK = 128
MLP_HEADS = 8
MLP_HEAD_DIM = GROUP_WIDTH // MLP_HEADS
D_FF = ((8 * D_MODEL // 3 + 255) // 256) * 256
EPS = 1e-6

kernel_name = "hybrid_rglru_chunkmlp_stream_step"


def rmsnorm(x, g):
    xf = x.astype(jnp.float32)
    y = xf * lax.rsqrt(jnp.mean(xf * xf, axis=-1, keepdims=True) + EPS)
    return (y * g.astype(jnp.float32)).astype(x.dtype)


def layernorm(x, g, b):
    xf = x.astype(jnp.float32)
    mu = jnp.mean(xf, axis=-1, keepdims=True)
    var = jnp.mean(jnp.square(xf - mu), axis=-1, keepdims=True)
    y = (xf - mu) * lax.rsqrt(var + EPS)
    return (y * g.astype(jnp.float32) + b.astype(jnp.float32)).astype(x.dtype)


def causal_conv(x, buf, w, b):
    T = x.shape[1]
    xp = jnp.concatenate([buf.astype(x.dtype), x], axis=1)
    y = sum(xp[:, k:k + T] * w[k] for k in range(CONV_WIDTH)) + b
    return y, xp[:, -(CONV_WIDTH - 1):]


def block_diag_linear(x, w, b):
    B, T, _ = x.shape
    xh = x.reshape(B, T, LRU_HEADS, LRU_HEAD_DIM)
    y = jnp.einsum('bthi,hij->bthj', xh, w).reshape(B, T, GROUP_WIDTH)
    return y + b


def rglru(x, r, i, lam, h0):
    xf, rf, if_ = (t.astype(jnp.float32) for t in (x, r, i))
    log_a = -LRU_C * rf * jax.nn.softplus(-lam.astype(jnp.float32))
    a = jnp.exp(log_a)
    mult = jnp.sqrt(-jnp.expm1(2.0 * log_a))
    bterm = mult * (if_ * xf)
    bterm = bterm.at[:, 0].add(a[:, 0] * h0.astype(jnp.float32))

    def combine(left, right):
        a1, b1 = left
        a2, b2 = right
        return a1 * a2, a2 * b1 + b2

    _, h = lax.associative_scan(combine, (a, bterm), axis=1)
    return h.astype(x.dtype), h[:, -1].astype(x.dtype)


def chunk_token_mlp(u, v, w_s, b_s):
    B, T, C = v.shape
    n = -(-T // MLP_CHUNK)
    pad = n * MLP_CHUNK - T
    vp = jnp.pad(v, ((0, 0), (0, pad), (0, 0))).reshape(B, n, MLP_CHUNK, MLP_HEADS, MLP_HEAD_DIM)
    mask = jnp.tril(jnp.ones((MLP_CHUNK, MLP_CHUNK), dtype=bool))
    w = jnp.where(mask[None], w_s, jnp.zeros_like(w_s))
    mixed = jnp.einsum('hts,bnshd->bnthd', w, vp) + b_s.T[None, None, :, :, None]
    mixed = mixed.reshape(B, n * MLP_CHUNK, C)[:, :T]
    return u * mixed


def layer(x, conv_buf, h0, norm1, w_in, conv_w, conv_b, w_rgate, b_rgate, w_igate, b_igate,
          lru_param, v_ln_g, v_ln_b, w_spatial, b_spatial, gn_a, gn_b, w_out,
          norm2, w_gate, w_up, w_down):
    xn = rmsnorm(x, norm1)
    proj = xn @ w_in
    xa, ga, u, v = jnp.split(proj, 4, axis=-1)
    xc, new_buf = causal_conv(xa, conv_buf, conv_w, conv_b)
    r = jax.nn.sigmoid(block_diag_linear(xc, w_rgate, b_rgate))
    ig = jax.nn.sigmoid(block_diag_linear(xc, w_igate, b_igate))
    y_lru, h_last = rglru(xc, r, ig, lru_param, h0)
    out_a = y_lru * jax.nn.gelu(ga)
    u_act = jax.nn.gelu(u)
    v_n = layernorm(jax.nn.gelu(v), v_ln_g, v_ln_b)
    out_b = chunk_token_mlp(u_act, v_n, w_spatial, b_spatial)
    mixed = jnp.concatenate([rmsnorm(out_a, gn_a), rmsnorm(out_b, gn_b)], axis=-1)
    h = x + mixed @ w_out
    hn = rmsnorm(h, norm2)
    y = h + (jax.nn.silu(hn @ w_gate) * (hn @ w_up)) @ w_down
    return y, new_buf, h_last, v_n


def setup_inputs(seed: int = 0) -> dict:
    key = jax.random.key(seed)
    ks = jax.random.split(key, 32)
    f32 = jnp.float32
    nrm = lambda k, s, sc: jax.random.normal(k, s, f32) * sc
    u0 = jax.random.uniform(ks[10], (DEPTH, GROUP_WIDTH), f32, 0.9, 0.999)
    a0 = u0 ** (1.0 / LRU_C)
    lru_param = jnp.log(a0) - jnp.log1p(-a0)
    return {
        "x_prompt": nrm(ks[0], (BATCH, SEQ, D_MODEL), 1.0),
        "x_sample": nrm(ks[1], (DEC_BATCH, DEC_SEQ, D_MODEL), 1.0),
        "state_conv": nrm(ks[2], (DEPTH, DEC_BATCH, CONV_WIDTH - 1, GROUP_WIDTH), 0.5),
        "state_lru": nrm(ks[3], (DEPTH, DEC_BATCH, GROUP_WIDTH), 0.5),
        "norm1": 1.0 + nrm(ks[4], (DEPTH, D_MODEL), 0.02),
        "w_in": nrm(ks[5], (DEPTH, D_MODEL, 4 * GROUP_WIDTH), D_MODEL ** -0.5),
        "conv_w": nrm(ks[6], (DEPTH, CONV_WIDTH, GROUP_WIDTH), CONV_WIDTH ** -0.5),
        "conv_b": nrm(ks[7], (DEPTH, GROUP_WIDTH), 0.01),
        "w_rgate": nrm(ks[8], (DEPTH, LRU_HEADS, LRU_HEAD_DIM, LRU_HEAD_DIM), LRU_HEAD_DIM ** -0.5),
        "b_rgate": nrm(ks[9], (DEPTH, GROUP_WIDTH), 0.01),
        "w_igate": nrm(ks[11], (DEPTH, LRU_HEADS, LRU_HEAD_DIM, LRU_HEAD_DIM), LRU_HEAD_DIM ** -0.5),
        "b_igate": nrm(ks[12], (DEPTH, GROUP_WIDTH), 0.01),
        "lru_param": lru_param,
        "v_ln_g": 1.0 + nrm(ks[13], (DEPTH, GROUP_WIDTH), 0.02),
        "v_ln_b": nrm(ks[14], (DEPTH, GROUP_WIDTH), 0.01),
        "w_spatial": nrm(ks[15], (DEPTH, MLP_HEADS, MLP_CHUNK, MLP_CHUNK), MLP_CHUNK ** -0.5),
        "b_spatial": 1.0 + nrm(ks[16], (DEPTH, MLP_HEADS, MLP_CHUNK), 0.1),
        "gn_a": 1.0 + nrm(ks[17], (DEPTH, GROUP_WIDTH), 0.02),
        "gn_b": 1.0 + nrm(ks[18], (DEPTH, GROUP_WIDTH), 0.02),
        "w_out": nrm(ks[19], (DEPTH, MIX_WIDTH, D_MODEL), MIX_WIDTH ** -0.5),
        "norm2": 1.0 + nrm(ks[20], (DEPTH, D_MODEL), 0.02),
        "w_gate": nrm(ks[21], (DEPTH, D_MODEL, D_FF), D_MODEL ** -0.5),
        "w_up": nrm(ks[22], (DEPTH, D_MODEL, D_FF), D_MODEL ** -0.5),
        "w_down": nrm(ks[23], (DEPTH, D_FF, D_MODEL), D_FF ** -0.5),
        "norm_f": 1.0 + nrm(ks[24], (D_MODEL,), 0.02),
    }


def reference(x_prompt, x_sample, state_conv, state_lru, norm1, w_in, conv_w, conv_b,
              w_rgate, b_rgate, w_igate, b_igate, lru_param, v_ln_g, v_ln_b,
              w_spatial, b_spatial, gn_a, gn_b, w_out, norm2, w_gate, w_up, w_down, norm_f):
    xp = x_prompt
    xs = x_sample
    buf_p = jnp.zeros((x_prompt.shape[0], CONV_WIDTH - 1, GROUP_WIDTH), x_prompt.dtype)
    h_p = jnp.zeros((x_prompt.shape[0], GROUP_WIDTH), x_prompt.dtype)
    conv_p, lru_p, conv_s, lru_s, vrows_s = [], [], [], [], []
    for l in range(DEPTH):
        params = (norm1[l], w_in[l], conv_w[l], conv_b[l], w_rgate[l], b_rgate[l],
                  w_igate[l], b_igate[l], lru_param[l], v_ln_g[l], v_ln_b[l],
                  w_spatial[l], b_spatial[l], gn_a[l], gn_b[l], w_out[l],
                  norm2[l], w_gate[l], w_up[l], w_down[l])
        xp, nb_p, nh_p, _ = layer(xp, buf_p, h_p, *params)
        xs, nb_s, nh_s, vn_s = layer(xs, state_conv[l], state_lru[l], *params)
        conv_p.append(nb_p)
        lru_p.append(nh_p)
        conv_s.append(nb_s)
        lru_s.append(nh_s)
        vrows_s.append(vn_s)
    y_prompt = rmsnorm(xp, norm_f)
    y_sample = rmsnorm(xs, norm_f)
    new_conv_prompt = jnp.stack(conv_p)
    new_lru_prompt = jnp.stack(lru_p)
    new_conv_sample = jnp.stack(conv_s)
    new_lru_sample = jnp.stack(lru_s)
    new_vrows_sample = jnp.stack(vrows_s)
    return (y_prompt, y_sample, new_conv_prompt, new_lru_prompt, new_conv_sample, new_lru_sample, new_vrows_sample)
```

```python
import numpy as np
import concourse.bass as bass
import concourse.mybir as mybir
from concourse.bass_utils import run_bass_kernel_spmd

F32 = mybir.dt.float32
BF16 = mybir.dt.bfloat16
AF = mybir.ActivationFunctionType
ALU = mybir.AluOpType

D = 2048
GW = 1024
DFF = 5632
L = 2
T = 512
NS = 128
EPS = 1e-6
NPAR = 240
SLOT = 4096
NSLOT = 6


class Sched:
    def __init__(self, nc, esem):
        self.nc = nc
        self.esem = esem
        self.ops = {e: [] for e in esem}
        self.cnt = {e: 0 for e in esem}
        self.prog = {e: [] for e in esem}
        self.lastw = {}
        self.readers = {}
        self.waited = {e: {} for e in esem}
        self.dcnt = {}

    def _resolve(self, ref, eng):
        if ref[0] == "dma":
            return (ref[1], ref[2])
        e, idx = ref
        if e == eng and e == "pe":
            return None
        lst = self.ops[e]
        for j in range(idx, len(lst)):
            if lst[j] is not None:
                return (self.esem[e], lst[j])
        raise RuntimeError(f"unsignaled dep on {e} idx {idx}")

    def op(self, eng, fn, reads=(), writes=(), signal=True, dma_sem=None):
        deps = []
        for k in reads:
            if k in self.lastw:
                deps.append(self.lastw[k])
        for k in writes:
            if k in self.lastw:
                deps.append(self.lastw[k])
            deps.extend(self.readers.get(k, ()))
        waits = {}
        for d in deps:
            r = self._resolve(d, eng)
            if r is None:
                continue
            s, v = r
            key = id(s)
            if self.waited[eng].get(key, 0) >= v:
                continue
            if key not in waits or waits[key][1] < v:
                waits[key] = (s, v)
        for key, (s, v) in waits.items():
            self.waited[eng][key] = v
        if dma_sem is not None:
            self.dcnt[id(dma_sem)] = self.dcnt.get(id(dma_sem), 0) + 16
            ref = ("dma", dma_sem, self.dcnt[id(dma_sem)])
            self.ops[eng].append(None)
            inc = (dma_sem, 16)
        else:
            if signal:
                self.cnt[eng] += 1
                self.ops[eng].append(self.cnt[eng])
                inc = (self.esem[eng], 1)
            else:
                self.ops[eng].append(None)
                inc = None
            ref = (eng, len(self.ops[eng]) - 1)
        self.prog[eng].append((list(waits.values()), fn, inc))
        for k in writes:
            self.lastw[k] = ref
            self.readers[k] = []
        for k in reads:
            if k not in writes:
                self.readers.setdefault(k, []).append(ref)

    def emit(self, eng, e):
        for waits, fn, inc in self.prog[eng]:
            for s, v in waits:
                e.wait_ge(s, v)
            ins = fn(e)
            if inc is not None:
                ins.then_inc(inc[0], inc[1])


def build_nc(NPT):
    NTOK = NPT * T + NS
    nc = bass.Bass("TRN2", target_bir_lowering=False)
    dt_in = lambda name, shape: nc.dram_tensor(name, shape, F32, kind="ExternalInput").ap()
    dt_out = lambda name, shape: nc.dram_tensor(name, shape, F32, kind="ExternalOutput").ap()
    xT = dt_in("xT", [D, NTOK])
    par = dt_in("par", [128, NPAR])
    w_in = dt_in("w_in", [L, D, 4 * GW])
    w_out = dt_in("w_out", [L, D, D])
    w_gate = dt_in("w_gate", [L, D, DFF])
    w_up = dt_in("w_up", [L, D, DFF])
    w_down = dt_in("w_down", [L, DFF, D])
    w_rg = dt_in("w_rg", [L, 8, 128, 128])
    w_ig = dt_in("w_ig", [L, 8, 128, 128])
    wsT = dt_in("wsT", [L, 128, 8, 128])
    bsp = dt_in("bsp", [L, 1, GW])
    lng = dt_in("lng", [L, GW])
    lnb = dt_in("lnb", [L, GW])
    sci = dt_in("sci", [128, L * 2 * 8 * 3])
    shi = dt_in("shi", [128, L * 2 * 8])
    yT = dt_out("yT", [D, NTOK])
    convP = dt_out("convP", [128, L * 8 * 3])
    lruP = dt_out("lruP", [128, L * 8])
    convS = dt_out("convS", [128, L * 2 * 8 * 3])
    lruS = dt_out("lruS", [128, L * 2 * 8])
    vrows = dt_out("vrows", [L, NS, GW])

    tiles = [(i * T, T, "P") for i in range(NPT)] + [(NPT * T, NS, "S")]
    dt_scr = lambda name, shape: nc.dram_tensor(name, shape, BF16, kind="Internal").ap()
    wb_in = dt_scr("wb_in", [L, D, 4 * GW])
    wb_out = dt_scr("wb_out", [L, D, D])
    wb_gate = dt_scr("wb_gate", [L, D, DFF])
    wb_up = dt_scr("wb_up", [L, D, DFF])
    wb_down = dt_scr("wb_down", [L, DFF, D])

    import contextlib
    with contextlib.ExitStack() as es:
        sb = lambda name, shape, dt=F32: es.enter_context(nc.sbuf_tensor(name, shape, dt))
        ps = lambda name: es.enter_context(nc.psum_tensor(name, [128, 512], F32))
        sem = lambda name: es.enter_context(nc.semaphore(name))
        X = sb("X", [128, 16, T])
        XN = sb("XN", [128, 16, T], BF16)
        OA = sb("OA", [128, 8, T])
        UB = sb("UB", [128, 8, T])
        slots = [sb(f"slot{i}", [128, SLOT], BF16) for i in range(NSLOT)]
        PAR = sb("PAR", [128, NPAR])
        CEXP = sb("CEXP", [128, L * 8])
        TMP8 = sb("TMP8", [128, 4, L * 8])
        WR = sb("WR", [128, L, 8, 128], BF16)
        WI = sb("WI", [128, L, 8, 128], BF16)
        WST = sb("WST", [128, L, 8, 128], BF16)
        WSS = sb("WSS", [128, L, 8, 128], BF16)
        ONES = sb("ONES", [128, 128])
        BROW = sb("BROW", [1, L, GW])
        LNG = sb("LNG", [128, GW])
        LNB = sb("LNB", [128, GW])
        CC = sb("CC", [128, L * 8 * 3])
        HC = sb("HC", [128, L * 8])
        SCI = sb("SCI", [128, L * 2 * 8 * 3])
        SHI = sb("SHI", [128, L * 2 * 8])
        SCO = sb("SCO", [128, L * 2 * 8 * 3])
        SHO = sb("SHO", [128, L * 2 * 8])
        XPs = [sb(f"XP{i}", [128, T + 8]) for i in range(2)]
        XCs = [sb(f"XC{i}", [128, T]) for i in range(2)]
        XCBs = [sb(f"XCB{i}", [128, T], BF16) for i in range(2)]
        RR = sb("RR", [128, T])
        II = sb("II", [128, T])
        AA = sb("AA", [128, T])
        MM = sb("MM", [128, T])
        SQ = [sb(f"SQ{i}", [128, T]) for i in range(2)]
        RST = sb("RST", [128, T])
        RSA = sb("RSA", [128, T])
        RSB = sb("RSB", [128, T])
        VF = sb("VF", [128, GW])
        WSF = VF[:, :].rearrange("p (h t) -> p h t", h=8)
        GG = MM
        VN = sb("VN", [128, 4, GW], BF16)
        BST = sb("BST", [128, 16])
        ACTB = sb("ACTB", [128, 4, T], BF16)
        psA = [ps("psA0"), ps("psA1")]
        psB = [ps("psB0"), ps("psB1")]
        psS = ps("psS")
        psV = [ps("psV0"), ps("psV1")]
        esem = {e: sem("s_" + e) for e in ("pe", "act", "dve", "pool", "sp")}
        slot_sem = [sem(f"sl{i}") for i in range(NSLOT)]
        x_sems = {0: sem("xs0"), 8: sem("xs8")}
        y_sems = {0: sem("ys0"), 8: sem("ys8")}
        c_sem = sem("cs")
        ln_sem = sem("lns")
        o_sem = sem("os")
        v_sem = sem("vs")
        S = Sched(nc, esem)
        csems = []

        def c_new():
            csems.append(sem(f"c{len(csems)}"))
            return csems[-1]
        lng_sem = sem("lng_s")
        lnb_sem = sem("lnb_s")
        nslot = [0]

        def load_w(src_ap, view, skey):
            i = nslot[0] % NSLOT
            nslot[0] += 1
            key = f"slot{i}"
            dst = view(slots[i])
            nk = dst.shape[1]
            step = 8 if nk > 8 else nk
            for k0 in range(0, nk, step):
                S.op("sp", lambda e, d=dst, s=src_ap, k0=k0: e.dma_start(out=d[:, k0:k0 + step, :], in_=s[:, k0:k0 + step, :]),
                     reads=[skey], writes=[key], dma_sem=slot_sem[i])
            return dst, key

        def precast(dst, src, skey, rows, piece):
            sm = sem("pc_" + skey)
            for r0 in range(0, rows, piece):
                S.op("pool", lambda e, r0=r0: e.dma_start(out=dst[r0:r0 + piece, :], in_=src[r0:r0 + piece, :]),
                     writes=[skey], dma_sem=sm)

        v16 = lambda sl: sl[:, :].rearrange("p (k n) -> p k n", k=16)
        v2 = lambda sl: sl[:, :].rearrange("p (k n) -> p k n", k=2)

        S.op("sp", lambda e: e.dma_start(out=PAR[:, :], in_=par[:, :]), writes=["PAR"], dma_sem=c_new())
        S.op("sp", lambda e: e.dma_start(out=SCI[:, :], in_=sci[:, :]), writes=["SCI"], dma_sem=c_new())
        S.op("sp", lambda e: e.dma_start(out=SHI[:, :], in_=shi[:, :]), writes=["SHI"], dma_sem=c_new())
        S.op("sp", lambda e: e.dma_start(out=BROW[:, :, :], in_=bsp.rearrange("l o n -> o l n")),
             writes=["BROW"], dma_sem=c_new())
        S.op("dve", lambda e: e.memset(ONES[:, :], 1.0), writes=["ONES"])
        S.op("dve", lambda e: e.memset(CC[:, :], 0.0), writes=["CC"])
        S.op("dve", lambda e: e.memset(HC[:, :], 0.0), writes=["HC"])
        S.op("dve", lambda e: e.memset(WSF[:, :, :], 0.0), writes=["VF", "VFb"])
        for l in range(L):
            S.op("pool", lambda e, l=l: e.dma_start(out=WR[:, l, :, :], in_=w_rg[l].rearrange("h i j -> i h j")),
                 writes=[f"WR{l}"], dma_sem=c_new())
            S.op("pool", lambda e, l=l: e.dma_start(out=WI[:, l, :, :], in_=w_ig[l].rearrange("h i j -> i h j")),
                 writes=[f"WI{l}"], dma_sem=c_new())
            S.op("pool", lambda e, l=l: e.dma_start(out=WST[:, l, :, :], in_=wsT[l]), writes=[f"WST{l}"], dma_sem=c_new())
            S.op("pool", lambda e, l=l: e.affine_select(
                out=WST[:, l, :, :], in_=WST[:, l, :, :], pattern=[[0, 8], [1, 128]],
                compare_op=ALU.is_ge, fill=0.0, base=0, channel_multiplier=-1),
                reads=[f"WST{l}"], writes=[f"WST{l}"])
            S.op("sp", lambda e, l=l: e.dma_start(out=WSF[0:64, :, 0:64], in_=wsT[l, 0:64, :, 0:64]),
                 writes=["VF"], dma_sem=c_new())
            S.op("sp", lambda e, l=l: e.dma_start(out=WSF[64:128, :, 64:128], in_=wsT[l, 0:64, :, 0:64]),
                 writes=["VFb"], dma_sem=c_new())
            S.op("pool", lambda e, l=l: e.affine_select(
                out=WSS[:, l, :, :], in_=WSF[:, :, :], pattern=[[0, 8], [1, 128]],
                compare_op=ALU.is_ge, fill=0.0, base=0, channel_multiplier=-1),
                reads=["VF", "VFb"], writes=[f"WSS{l}"])
        for l in range(L):
            lam = PAR[:, l * 112 + 88:l * 112 + 96]
            c = slice(l * 8, l * 8 + 8)
            S.op("dve", lambda e, lam=lam, c=c: e.tensor_scalar(
                out=TMP8[:, 0, c], in0=lam, scalar1=-1.0, scalar2=None, op0=ALU.mult),
                reads=["PAR"], writes=["T0"])
            S.op("act", lambda e, c=c: e.activation(out=TMP8[:, 1, c], in_=TMP8[:, 0, c], func=AF.Abs),
                 reads=["T0"], writes=["T1"])
            S.op("act", lambda e, c=c: e.activation(out=TMP8[:, 2, c], in_=TMP8[:, 1, c], func=AF.Exp, scale=-1.0),
                 reads=["T1"], writes=["T2"])
            S.op("act", lambda e, c=c: e.activation(out=TMP8[:, 3, c], in_=TMP8[:, 2, c], func=AF.Ln, bias=1.0),
                 reads=["T2"], writes=["T3"])
            S.op("dve", lambda e, c=c: e.tensor_scalar(
                out=TMP8[:, 0, c], in0=TMP8[:, 0, c], scalar1=0.0, scalar2=None, op0=ALU.max),
                reads=["T0"], writes=["T0"])
            S.op("dve", lambda e, c=c: e.tensor_tensor(out=TMP8[:, 0, c], in0=TMP8[:, 0, c], in1=TMP8[:, 3, c], op=ALU.add),
                 reads=["T0", "T3"], writes=["T0"])
            S.op("dve", lambda e, c=c: e.tensor_scalar(
                out=CEXP[:, c], in0=TMP8[:, 0, c], scalar1=-8.0, scalar2=None, op0=ALU.mult),
                reads=["T0"], writes=["CEXP"])

        for l in range(L):
            precast(wb_in[l], w_in[l], f"wb_in{l}", D, 512)
            precast(wb_out[l], w_out[l], f"wb_out{l}", D, 512)
            precast(wb_gate[l], w_gate[l], f"wb_gate{l}", D, 512)
            precast(wb_up[l], w_up[l], f"wb_up{l}", D, 512)
            precast(wb_down[l], w_down[l], f"wb_down{l}", DFF, 512)

        sqi = [0]

        def rms_stats(src_fn, keys, nchunks, n, dim, out_rstd, out_key):
            for kc in range(nchunks):
                b = sqi[0] % 2
                sqi[0] += 1
                S.op("act", lambda e, kc=kc, b=b: e.activation(out=SQ[b][:, :n], in_=src_fn(kc), func=AF.Square),
                     reads=[keys(kc)], writes=[f"SQ{b}"])
                S.op("pe", lambda e, kc=kc, b=b: e.matmul(psS[:, :n], lhsT=ONES[:, :], rhs=SQ[b][:, :n],
                                                          start=(kc == 0), stop=(kc == nchunks - 1)),
                     reads=[f"SQ{b}", "ONES"], writes=["psS"], signal=True)
            S.op("act", lambda e: e.activation(out=out_rstd[:, :n], in_=psS[:, :n], func=AF.Sqrt,
                                               scale=1.0 / dim, bias=EPS),
                 reads=["psS"], writes=[out_key])
            S.op("dve", lambda e: e.reciprocal(out=out_rstd[:, :n], in_=out_rstd[:, :n]),
                 reads=[out_key], writes=[out_key])

        def norm_to_bf16(n, gcol, rstd, rkey):
            for kc in range(16):
                S.op("dve", lambda e, kc=kc: e.scalar_tensor_tensor(
                    out=XN[:, kc, :n], in0=X[:, kc, :n], scalar=PAR[:, gcol + kc:gcol + kc + 1],
                    in1=rstd[:, :n], op0=ALU.mult, op1=ALU.mult),
                    reads=[f"X{kc}", rkey, "PAR"], writes=[f"XN{kc}"])

        def proj_chunk(pst, pkey, wv, wkey, c0, n):
            for kc in range(16):
                S.op("pe", lambda e, kc=kc: e.matmul(pst[:, :n], lhsT=wv[:, kc, c0:c0 + 128], rhs=XN[:, kc, :n],
                                                     start=(kc == 0), stop=(kc == 15)),
                     reads=[wkey, f"XN{kc}"], writes=[pkey], signal=(kc == 15))

        def layer(l, n, kind, col0):
            pb = l * 112
            segs = [(0, n)] if kind == "P" else [(0, 64), (64, 64)]
            nb = n // 128
            S.op("sp", lambda e: e.dma_start(out=LNG[:, :], in_=lng[l].partition_broadcast(128)),
                 writes=["LNG"], dma_sem=lng_sem)
            S.op("sp", lambda e: e.dma_start(out=LNB[:, :], in_=lnb[l].partition_broadcast(128)),
                 writes=["LNB"], dma_sem=lnb_sem)
            rms_stats(lambda kc: X[:, kc, :n], lambda kc: f"X{kc}", 16, n, D, RST, "RST")
            norm_to_bf16(n, pb + 0, RST, "RST")
            wv_in = wb_in[l].rearrange("(k p) c -> p k c", p=128)
            pi = [0]

            def nextA():
                pi[0] += 1
                return psA[pi[0] % 2], f"psA{pi[0] % 2}"
            xa_w = {}

            def stage1(h):
                blk, hh = h // 2, h % 2
                if hh == 0:
                    xa_w[blk] = load_w(wv_in[:, :, blk * 256:(blk + 1) * 256], v16, f"wb_in{l}")
                wv, wkey = xa_w[blk]
                bi = h % 2
                XP, XC, XCB = XPs[bi], XCs[bi], XCBs[bi]
                kXP, kXC, kXCB = f"XP{bi}", f"XC{bi}", f"XCB{bi}"
                pst, pkey = nextA()
                proj_chunk(pst, pkey, wv, wkey, hh * 128, n)
                for si, (c0, sn) in enumerate(segs):
                    off = c0 + 3 * si
                    if kind == "P":
                        cin = CC[:, (l * 8 + h) * 3:(l * 8 + h) * 3 + 3]
                        cout = cin
                        cik, cok = "CC", "CC"
                    else:
                        o3 = ((l * 2 + si) * 8 + h) * 3
                        cin, cout = SCI[:, o3:o3 + 3], SCO[:, o3:o3 + 3]
                        cik, cok = "SCI", "SCO"
                    S.op("dve", lambda e, off=off, cin=cin: e.tensor_copy(out=XP[:, off:off + 3], in_=cin),
                         reads=[cik], writes=[kXP])
                    S.op("act", lambda e, off=off, c0=c0, sn=sn, pst=pst: e.activation(
                        out=XP[:, off + 3:off + 3 + sn], in_=pst[:, c0:c0 + sn], func=AF.Copy),
                        reads=[pkey], writes=[kXP])
                    w0 = pb + 32 + 0 * 8 + h
                    S.op("dve", lambda e, off=off, c0=c0, sn=sn, w0=w0, h=h: e.tensor_scalar(
                        out=XC[:, c0:c0 + sn], in0=XP[:, off:off + sn], scalar1=PAR[:, w0:w0 + 1],
                        scalar2=PAR[:, pb + 64 + h:pb + 65 + h], op0=ALU.mult, op1=ALU.add),
                        reads=[kXP, "PAR"], writes=[kXC])
                    for k in range(1, 4):
                        wk = pb + 32 + k * 8 + h
                        S.op("dve", lambda e, off=off, c0=c0, sn=sn, wk=wk, k=k: e.scalar_tensor_tensor(
                            out=XC[:, c0:c0 + sn], in0=XP[:, off + k:off + k + sn], scalar=PAR[:, wk:wk + 1],
                            in1=XC[:, c0:c0 + sn], op0=ALU.mult, op1=ALU.add),
                            reads=[kXP, kXC, "PAR"], writes=[kXC])
                    S.op("dve", lambda e, off=off, sn=sn, cout=cout: e.tensor_copy(
                        out=cout, in_=XP[:, off + sn:off + sn + 3]),
                        reads=[kXP], writes=[cok])
                S.op("act", lambda e: e.activation(out=XCB[:, :n], in_=XC[:, :n], func=AF.Copy),
                     reads=[kXC], writes=[kXCB])

            def stage2(h):
                bi = h % 2
                XC, XCB = XCs[bi], XCBs[bi]
                kXC, kXCB = f"XC{bi}", f"XCB{bi}"
                S.op("pe", lambda e: e.matmul(psB[0][:, :n], lhsT=WR[:, l, h, :], rhs=XCB[:, :n], start=True, stop=True),
                     reads=[kXCB, f"WR{l}"], writes=["psB0"])
                S.op("pe", lambda e: e.matmul(psB[1][:, :n], lhsT=WI[:, l, h, :], rhs=XCB[:, :n], start=True, stop=True),
                     reads=[kXCB, f"WI{l}"], writes=["psB1"])
                S.op("act", lambda e: e.activation(out=RR[:, :n], in_=psB[0][:, :n], func=AF.Sigmoid,
                                                   bias=PAR[:, pb + 72 + h:pb + 73 + h]),
                     reads=["psB0", "PAR"], writes=["RR"])
                S.op("act", lambda e: e.activation(out=II[:, :n], in_=psB[1][:, :n], func=AF.Sigmoid,
                                                   bias=PAR[:, pb + 80 + h:pb + 81 + h]),
                     reads=["psB1", "PAR"], writes=["II"])
                S.op("act", lambda e: e.activation(out=AA[:, :n], in_=RR[:, :n], func=AF.Exp,
                                                   scale=CEXP[:, l * 8 + h:l * 8 + h + 1]),
                     reads=["RR", "CEXP"], writes=["AA"])
                S.op("dve", lambda e: e.tensor_tensor(out=MM[:, :n], in0=AA[:, :n], in1=AA[:, :n], op=ALU.mult),
                     reads=["AA"], writes=["MM"])
                S.op("act", lambda e: e.activation(out=MM[:, :n], in_=MM[:, :n], func=AF.Sqrt, scale=-1.0, bias=1.0),
                     reads=["MM"], writes=["MM"])
                S.op("dve", lambda e: e.tensor_tensor(out=II[:, :n], in0=II[:, :n], in1=XC[:, :n], op=ALU.mult),
                     reads=["II", kXC], writes=["II"])
                S.op("dve", lambda e: e.tensor_tensor(out=II[:, :n], in0=II[:, :n], in1=MM[:, :n], op=ALU.mult),
                     reads=["II", "MM"], writes=["II"])
                for si, (c0, sn) in enumerate(segs):
                    if kind == "P":
                        hin = HC[:, l * 8 + h:l * 8 + h + 1]; hout = hin; hik = hok = "HC"
                    else:
                        o1 = (l * 2 + si) * 8 + h
                        hin, hout = SHI[:, o1:o1 + 1], SHO[:, o1:o1 + 1]; hik, hok = "SHI", "SHO"
                    S.op("dve", lambda e, c0=c0, sn=sn, hin=hin: e.tensor_tensor_scan(
                        out=OA[:, h, c0:c0 + sn], data0=AA[:, c0:c0 + sn], data1=II[:, c0:c0 + sn],
                        initial=hin, op0=ALU.mult, op1=ALU.add),
                        reads=["AA", "II", hik], writes=[f"OA{h}"])
                    S.op("dve", lambda e, c0=c0, sn=sn, hout=hout: e.tensor_copy(
                        out=hout, in_=OA[:, h, c0 + sn - 1:c0 + sn]),
                        reads=[f"OA{h}"], writes=[hok])

            stage1(0)
            for h in range(8):
                if h + 1 < 8:
                    stage1(h + 1)
                stage2(h)
            for blk in range(4):
                wv, wkey = load_w(wv_in[:, :, GW + blk * 256:GW + (blk + 1) * 256], v16, f"wb_in{l}")
                for hh in range(2):
                    h = blk * 2 + hh
                    pst, pkey = nextA()
                    proj_chunk(pst, pkey, wv, wkey, hh * 128, n)
                    S.op("act", lambda e, pst=pst: e.activation(out=GG[:, :n], in_=pst[:, :n], func=AF.Gelu_apprx_tanh),
                         reads=[pkey], writes=["MM"])
                    S.op("dve", lambda e, h=h: e.tensor_tensor(out=OA[:, h, :n], in0=OA[:, h, :n], in1=GG[:, :n], op=ALU.mult),
                         reads=["MM", f"OA{h}"], writes=[f"OA{h}"])
                    b = sqi[0] % 2
                    sqi[0] += 1
                    S.op("act", lambda e, h=h, b=b: e.activation(out=SQ[b][:, :n], in_=OA[:, h, :n], func=AF.Square),
                         reads=[f"OA{h}"], writes=[f"SQ{b}"])
                    S.op("pe", lambda e, h=h, b=b: e.matmul(psS[:, :n], lhsT=ONES[:, :], rhs=SQ[b][:, :n],
                                                            start=(h == 0), stop=(h == 7)),
                         reads=[f"SQ{b}", "ONES"], writes=["psS"])
            S.op("act", lambda e: e.activation(out=RSA[:, :n], in_=psS[:, :n], func=AF.Sqrt, scale=1.0 / GW, bias=EPS),
                 reads=["psS"], writes=["RSA"])
            S.op("dve", lambda e: e.reciprocal(out=RSA[:, :n], in_=RSA[:, :n]), reads=["RSA"], writes=["RSA"])
            for blk in range(4):
                wv, wkey = load_w(wv_in[:, :, 2 * GW + blk * 256:2 * GW + (blk + 1) * 256], v16, f"wb_in{l}")
                for hh in range(2):
                    h = blk * 2 + hh
                    pst, pkey = nextA()
                    proj_chunk(pst, pkey, wv, wkey, hh * 128, n)
                    S.op("act", lambda e, pst=pst, h=h: e.activation(out=UB[:, h, :n], in_=pst[:, :n], func=AF.Gelu_apprx_tanh),
                         reads=[pkey], writes=[f"UB{h}"])
            vw = [load_w(wv_in[:, :, 3 * GW + blk * 256:3 * GW + (blk + 1) * 256], v16, f"wb_in{l}") for blk in range(4)]
            for tb in range(nb):
                for half in range(2):
                    for q in range(2):
                        wv, wkey = vw[half * 2 + q]
                        for kc in range(16):
                            S.op("pe", lambda e, kc=kc, wv=wv, half=half, q=q, tb=tb: e.matmul(
                                psV[half][:, q * 256:(q + 1) * 256], lhsT=XN[:, kc, tb * 128:(tb + 1) * 128],
                                rhs=wv[:, kc, :], start=(kc == 0), stop=(kc == 15)),
                                reads=[wkey, f"XN{kc}"], writes=[f"psV{half}"], signal=(kc == 15))
                    S.op("act", lambda e, half=half: e.activation(
                        out=VF[:, half * 512:(half + 1) * 512], in_=psV[half][:, :], func=AF.Gelu_apprx_tanh),
                        reads=[f"psV{half}"], writes=["VF"])
                    S.op("dve", lambda e, half=half: e.bn_stats(out=BST[:, half * 6:half * 6 + 6],
                                                                in_=VF[:, half * 512:(half + 1) * 512]),
                         reads=["VF"], writes=["BST"])
                S.op("dve", lambda e: e.bn_aggr(out=BST[:, 12:14], in_=BST[:, 0:12]), reads=["BST"], writes=["BST"])
                S.op("act", lambda e: e.activation(out=BST[:, 14:15], in_=BST[:, 13:14], func=AF.Sqrt, bias=EPS),
                     reads=["BST"], writes=["BST"])
                S.op("dve", lambda e: e.reciprocal(out=BST[:, 14:15], in_=BST[:, 14:15]), reads=["BST"], writes=["BST"])
                S.op("dve", lambda e: e.tensor_scalar(out=VF[:, :], in0=VF[:, :], scalar1=BST[:, 12:13],
                                                      scalar2=BST[:, 14:15], op0=ALU.subtract, op1=ALU.mult),
                     reads=["VF", "BST"], writes=["VF"])
                S.op("dve", lambda e: e.tensor_tensor(out=VF[:, :], in0=VF[:, :], in1=LNG[:, :], op=ALU.mult),
                     reads=["VF", "LNG"], writes=["VF"])
                S.op("dve", lambda e: e.tensor_tensor(out=VF[:, :], in0=VF[:, :], in1=LNB[:, :], op=ALU.add),
                     reads=["VF", "LNB"], writes=["VF"])
                S.op("act", lambda e, tb=tb: e.activation(out=VN[:, tb, :], in_=VF[:, :], func=AF.Copy),
                     reads=["VF"], writes=["VN"])
                if kind == "S":
                    S.op("sp", lambda e: e.dma_start(out=vrows[l], in_=VF[:, :]), reads=["VF"], writes=["vrows"],
                         dma_sem=v_sem)
            wmix = WST if kind == "P" else WSS
            for h in range(8):
                pst, pkey = nextA()
                for tb in range(nb):
                    S.op("pe", lambda e, h=h, tb=tb, pst=pst: e.matmul(
                        pst[:, tb * 128:(tb + 1) * 128], lhsT=VN[:, tb, h * 128:(h + 1) * 128], rhs=wmix[:, l, h, :],
                        start=True, stop=False),
                        reads=["VN", f"WST{l}", f"WSS{l}"], writes=[pkey], signal=False)
                    if kind == "P":
                        S.op("pe", lambda e, h=h, tb=tb, pst=pst: e.matmul(
                            pst[:, tb * 128:(tb + 1) * 128], lhsT=ONES[0:1, :],
                            rhs=BROW[0:1, l, h * 128:h * 128 + 128], start=False, stop=True),
                            reads=["BROW", "ONES"], writes=[pkey], signal=(tb == nb - 1))
                    else:
                        for half in range(2):
                            S.op("pe", lambda e, h=h, half=half, pst=pst: e.matmul(
                                pst[:, half * 64:half * 64 + 64], lhsT=ONES[0:1, :],
                                rhs=BROW[0:1, l, h * 128:h * 128 + 64], start=False, stop=True),
                                reads=["BROW", "ONES"], writes=[pkey], signal=(half == 1))
                S.op("dve", lambda e, h=h, pst=pst: e.tensor_tensor(out=UB[:, h, :n], in0=pst[:, :n], in1=UB[:, h, :n], op=ALU.mult),
                     reads=[pkey, f"UB{h}"], writes=[f"UB{h}"])
                b = sqi[0] % 2
                sqi[0] += 1
                S.op("act", lambda e, h=h, b=b: e.activation(out=SQ[b][:, :n], in_=UB[:, h, :n], func=AF.Square),
                     reads=[f"UB{h}"], writes=[f"SQ{b}"])
                S.op("pe", lambda e, h=h, b=b: e.matmul(psS[:, :n], lhsT=ONES[:, :], rhs=SQ[b][:, :n],
                                                        start=(h == 0), stop=(h == 7)),
                     reads=[f"SQ{b}", "ONES"], writes=["psS"])
            S.op("act", lambda e: e.activation(out=RSB[:, :n], in_=psS[:, :n], func=AF.Sqrt, scale=1.0 / GW, bias=EPS),
                 reads=["psS"], writes=["RSB"])
            S.op("dve", lambda e: e.reciprocal(out=RSB[:, :n], in_=RSB[:, :n]), reads=["RSB"], writes=["RSB"])
            for h in range(8):
                S.op("dve", lambda e, h=h: e.scalar_tensor_tensor(
                    out=XN[:, h, :n], in0=OA[:, h, :n], scalar=PAR[:, pb + 96 + h:pb + 97 + h], in1=RSA[:, :n],
                    op0=ALU.mult, op1=ALU.mult), reads=[f"OA{h}", "RSA", "PAR"], writes=[f"XN{h}"])
                S.op("dve", lambda e, h=h: e.scalar_tensor_tensor(
                    out=XN[:, 8 + h, :n], in0=UB[:, h, :n], scalar=PAR[:, pb + 104 + h:pb + 105 + h], in1=RSB[:, :n],
                    op0=ALU.mult, op1=ALU.mult), reads=[f"UB{h}", "RSB", "PAR"], writes=[f"XN{8 + h}"])
            wv_o = wb_out[l].rearrange("(k p) c -> p k c", p=128)
            for blk in range(8):
                wv, wkey = load_w(wv_o[:, :, blk * 256:(blk + 1) * 256], v16, f"wb_out{l}")
                for hh in range(2):
                    oc = blk * 2 + hh
                    pst, pkey = nextA()
                    proj_chunk(pst, pkey, wv, wkey, hh * 128, n)
                    S.op("dve", lambda e, oc=oc, pst=pst: e.tensor_tensor(out=X[:, oc, :n], in0=pst[:, :n], in1=X[:, oc, :n], op=ALU.add),
                         reads=[pkey, f"X{oc}"], writes=[f"X{oc}"])
            rms_stats(lambda kc: X[:, kc, :n], lambda kc: f"X{kc}", 16, n, D, RST, "RST")
            norm_to_bf16(n, pb + 16, RST, "RST")
            wv_g = wb_gate[l].rearrange("(k p) c -> p k c", p=128)
            wv_u = wb_up[l].rearrange("(k p) c -> p k c", p=128)
            wv_d = wb_down[l].rearrange("(k p) c -> p k c", p=128)
            NST = DFF // 256
            dslots = {}

            def ffn_gu(st):
                gv, gk = load_w(wv_g[:, :, st * 256:(st + 1) * 256], v16, f"wb_gate{l}")
                uv, uk = load_w(wv_u[:, :, st * 256:(st + 1) * 256], v16, f"wb_up{l}")
                dslots[st] = load_w(wv_d[:, st * 2:st * 2 + 2, :], v2, f"wb_down{l}")
                for j in range(2):
                    a = (st % 2) * 2 + j
                    proj_chunk(psA[j], f"psA{j}", gv, gk, j * 128, n)
                    proj_chunk(psB[j], f"psB{j}", uv, uk, j * 128, n)
                    S.op("act", lambda e, j=j: e.activation(out=GG[:, :n], in_=psA[j][:, :n], func=AF.Silu),
                         reads=[f"psA{j}"], writes=["MM"])
                    S.op("dve", lambda e, j=j, a=a: e.tensor_tensor(out=ACTB[:, a, :n], in0=psB[j][:, :n], in1=GG[:, :n], op=ALU.mult),
                         reads=[f"psB{j}", "MM"], writes=[f"ACTB{a}"])

            def ffn_d(st):
                dv, dk = dslots.pop(st)
                for oc in range(16):
                    pv = psV[oc % 2]
                    pk = f"psV{oc % 2}"
                    for j in range(2):
                        a = (st % 2) * 2 + j
                        S.op("pe", lambda e, oc=oc, j=j, pv=pv, a=a: e.matmul(
                            pv[:, :n], lhsT=dv[:, j, oc * 128:(oc + 1) * 128], rhs=ACTB[:, a, :n],
                            start=(j == 0), stop=(j == 1)),
                            reads=[dk, f"ACTB{a}"], writes=[pk], signal=(j == 1))
                    S.op("dve", lambda e, oc=oc, pv=pv: e.tensor_tensor(out=X[:, oc, :n], in0=pv[:, :n], in1=X[:, oc, :n], op=ALU.add),
                         reads=[pk, f"X{oc}"], writes=[f"X{oc}"])

            for st in range(NST):
                ffn_gu(st)
                if st >= 1:
                    ffn_d(st - 1)
            ffn_d(NST - 1)

        xTv = xT.rearrange("(k p) t -> p k t", p=128)
        yTv = yT.rearrange("(k p) t -> p k t", p=128)
        for (col0, n, kind) in tiles:
            for k0 in (0, 8):
                S.op("sp", lambda e, col0=col0, n=n, k0=k0: e.dma_start(
                    out=X[:, k0:k0 + 8, :n], in_=xTv[:, k0:k0 + 8, col0:col0 + n]),
                    writes=[f"X{kc}" for kc in range(k0, k0 + 8)], dma_sem=x_sems[k0])
            for l in range(L):
                layer(l, n, kind, col0)
            rms_stats(lambda kc, n=n: X[:, kc, :n], lambda kc: f"X{kc}", 16, n, D, RST, "RST")
            for kc in range(16):
                S.op("dve", lambda e, kc=kc, n=n: e.scalar_tensor_tensor(
                    out=X[:, kc, :n], in0=X[:, kc, :n], scalar=PAR[:, 224 + kc:225 + kc], in1=RST[:, :n],
                    op0=ALU.mult, op1=ALU.mult), reads=[f"X{kc}", "RST", "PAR"], writes=[f"X{kc}"])
            for k0 in (0, 8):
                S.op("sp", lambda e, col0=col0, n=n, k0=k0: e.dma_start(
                    out=yTv[:, k0:k0 + 8, col0:col0 + n], in_=X[:, k0:k0 + 8, :n]),
                    reads=[f"X{kc}" for kc in range(k0, k0 + 8)], writes=[f"yT{k0}"], dma_sem=y_sems[k0])
        S.op("sp", lambda e: e.dma_start(out=convP[:, :], in_=CC[:, :]), reads=["CC"], writes=["o1"], dma_sem=o_sem)
        S.op("sp", lambda e: e.dma_start(out=lruP[:, :], in_=HC[:, :]), reads=["HC"], writes=["o2"], dma_sem=o_sem)
        S.op("sp", lambda e: e.dma_start(out=convS[:, :], in_=SCO[:, :]), reads=["SCO"], writes=["o3"], dma_sem=o_sem)
        S.op("sp", lambda e: e.dma_start(out=lruS[:, :], in_=SHO[:, :]), reads=["SHO"], writes=["o4"], dma_sem=o_sem)
        n_y = len(tiles) * 16
        n_o = 4 * 16
        n_v = L * 16

        with nc.Block() as block:
            @block.sync
            def _(e):
                S.emit("sp", e)
                e.wait_ge(y_sems[0], n_y)
                e.wait_ge(y_sems[8], n_y)
                e.wait_ge(o_sem, n_o)
                e.wait_ge(v_sem, n_v)

            @block.gpsimd
            def _(e):
                S.emit("pool", e)

            @block.scalar
            def _(e):
                S.emit("act", e)

            @block.vector
            def _(e):
                S.emit("dve", e)

            @block.tensor
            def _(e):
                S.emit("pe", e)
    return nc


def _vec_pk(v, k):
    return np.ascontiguousarray(v.reshape(k, 128).T)


def run(NPT, x_prompt, x_sample, state_conv, state_lru, norm1, w_in, conv_w, conv_b,
        w_rgate, b_rgate, w_igate, b_igate, lru_param, v_ln_g, v_ln_b,
        w_spatial, b_spatial, gn_a, gn_b, w_out, norm2, w_gate, w_up, w_down, norm_f):
    f = lambda a: np.ascontiguousarray(np.asarray(a, dtype=np.float32))
    x_prompt, x_sample, state_conv, state_lru = map(f, (x_prompt, x_sample, state_conv, state_lru))
    B, SEQ, _ = x_prompt.shape
    assert SEQ == NPT * T
    NTOK = SEQ + NS
    par = np.zeros((128, NPAR), np.float32)
    for l in range(L):
        pb = l * 112
        par[:, pb:pb + 16] = _vec_pk(f(norm1[l]), 16)
        par[:, pb + 16:pb + 32] = _vec_pk(f(norm2[l]), 16)
        for k in range(4):
            par[:, pb + 32 + k * 8:pb + 40 + k * 8] = _vec_pk(f(conv_w[l, k]), 8)
        par[:, pb + 64:pb + 72] = _vec_pk(f(conv_b[l]), 8)
        par[:, pb + 72:pb + 80] = _vec_pk(f(b_rgate[l]), 8)
        par[:, pb + 80:pb + 88] = _vec_pk(f(b_igate[l]), 8)
        par[:, pb + 88:pb + 96] = _vec_pk(f(lru_param[l]), 8)
        par[:, pb + 96:pb + 104] = _vec_pk(f(gn_a[l]), 8)
        par[:, pb + 104:pb + 112] = _vec_pk(f(gn_b[l]), 8)
    par[:, 224:240] = _vec_pk(f(norm_f), 16)
    wsT = np.ascontiguousarray(f(w_spatial).transpose(0, 3, 1, 2))
    bsp = np.ascontiguousarray(f(b_spatial).reshape(L, 1, GW))
    shared = dict(par=par, w_in=f(w_in), w_out=f(w_out), w_gate=f(w_gate), w_up=f(w_up), w_down=f(w_down),
                  w_rg=f(w_rgate), w_ig=f(w_igate), wsT=wsT, bsp=bsp, lng=f(v_ln_g), lnb=f(v_ln_b))
    in_maps = []
    for c in range(8):
        xT = np.zeros((D, NTOK), np.float32)
        if c < B:
            xT[:, :SEQ] = x_prompt[c].T
        xs = x_sample[2 * c:2 * c + 2].reshape(NS, D)
        xT[:, SEQ:] = xs.T
        sc = state_conv[:, 2 * c:2 * c + 2]
        sci = sc.reshape(L, 2, 3, 8, 128).transpose(4, 0, 1, 3, 2).reshape(128, L * 2 * 8 * 3)
        sh = state_lru[:, 2 * c:2 * c + 2]
        shi = sh.reshape(L, 2, 8, 128).transpose(3, 0, 1, 2).reshape(128, L * 2 * 8)
        m = dict(shared)
        m.update(xT=xT, sci=np.ascontiguousarray(sci), shi=np.ascontiguousarray(shi))
        in_maps.append(m)
    nc = build_nc(NPT)
    res = run_bass_kernel_spmd(nc, in_maps, core_ids=list(range(8)))
    R = res.results
    y_prompt = np.stack([R[c]["yT"][:, :SEQ].T for c in range(B)]).astype(np.float32)
    y_sample = np.concatenate([R[c]["yT"][:, SEQ:].T.reshape(2, 64, D) for c in range(8)]).astype(np.float32)
    ncp = np.stack([R[c]["convP"].reshape(128, L, 8, 3).transpose(1, 3, 2, 0).reshape(L, 3, GW) for c in range(B)], axis=1)
    nlp = np.stack([R[c]["lruP"].reshape(128, L, 8).transpose(1, 2, 0).reshape(L, GW) for c in range(B)], axis=1)
    ncs = np.concatenate([R[c]["convS"].reshape(128, L, 2, 8, 3).transpose(1, 2, 4, 3, 0).reshape(L, 2, 3, GW)
                          for c in range(8)], axis=1)
    nls = np.concatenate([R[c]["lruS"].reshape(128, L, 2, 8).transpose(1, 2, 3, 0).reshape(L, 2, GW)
                          for c in range(8)], axis=1)
    nvs = np.concatenate([R[c]["vrows"].reshape(L, 2, 64, GW) for c in range(8)], axis=1)
    out = (y_prompt, y_sample, ncp, nlp, ncs, nls, nvs)
    return tuple(np.ascontiguousarray(o, dtype=np.float32) for o in out)


def kernel(**inputs):
    return run(16, **inputs)
```

```python
import numpy as np
import concourse.bass as bass
import concourse.mybir as mybir
from concourse.bass_utils import run_bass_kernel_spmd

F32 = mybir.dt.float32
BF16 = mybir.dt.bfloat16
AF = mybir.ActivationFunctionType
ALU = mybir.AluOpType

D = 2048
GW = 1024
DFF = 5632
L = 2
T = 512
NS = 128
EPS = 1e-6
NPAR = 240
SLOT = 4096
NSLOT = 6


class Sched:
    def __init__(self, nc, esem):
        self.nc = nc
        self.esem = esem
        self.ops = {e: [] for e in esem}
        self.cnt = {e: 0 for e in esem}
        self.prog = {e: [] for e in esem}
        self.lastw = {}
        self.readers = {}
        self.waited = {e: {} for e in esem}
        self.dcnt = {}

    def _resolve(self, ref, eng):
        if ref[0] == "dma":
            return (ref[1], ref[2])
        e, idx = ref
        if e == eng and e == "pe":
            return None
        lst = self.ops[e]
        for j in range(idx, len(lst)):
            if lst[j] is not None:
                return (self.esem[e], lst[j])
        raise RuntimeError(f"unsignaled dep on {e} idx {idx}")

    def op(self, eng, fn, reads=(), writes=(), signal=True, dma_sem=None):
        deps = []
        for k in reads:
            if k in self.lastw:
                deps.append(self.lastw[k])
        for k in writes:
            if k in self.lastw:
                deps.append(self.lastw[k])
            deps.extend(self.readers.get(k, ()))
        waits = {}
        for d in deps:
            r = self._resolve(d, eng)
            if r is None:
                continue
            s, v = r
            key = id(s)
            if self.waited[eng].get(key, 0) >= v:
                continue
            if key not in waits or waits[key][1] < v:
                waits[key] = (s, v)
        for key, (s, v) in waits.items():
            self.waited[eng][key] = v
        if dma_sem is not None:
            self.dcnt[id(dma_sem)] = self.dcnt.get(id(dma_sem), 0) + 16
            ref = ("dma", dma_sem, self.dcnt[id(dma_sem)])
            self.ops[eng].append(None)
            inc = (dma_sem, 16)
        else:
            if signal:
                self.cnt[eng] += 1
                self.ops[eng].append(self.cnt[eng])
                inc = (self.esem[eng], 1)
            else:
                self.ops[eng].append(None)
                inc = None
            ref = (eng, len(self.ops[eng]) - 1)
        self.prog[eng].append((list(waits.values()), fn, inc))
        for k in writes:
            self.lastw[k] = ref
            self.readers[k] = []
        for k in reads:
            if k not in writes:
                self.readers.setdefault(k, []).append(ref)

    def emit(self, eng, e):
        for waits, fn, inc in self.prog[eng]:
            for s, v in waits:
                e.wait_ge(s, v)
            ins = fn(e)
            if inc is not None:
                ins.then_inc(inc[0], inc[1])


def build_nc(NPT):
    NTOK = NPT * T + NS
    nc = bass.Bass("TRN2", target_bir_lowering=False)
    dt_in = lambda name, shape: nc.dram_tensor(name, shape, F32, kind="ExternalInput").ap()
    dt_out = lambda name, shape: nc.dram_tensor(name, shape, F32, kind="ExternalOutput").ap()
    xT = dt_in("xT", [D, NTOK])
    par = dt_in("par", [128, NPAR])
    w_in = dt_in("w_in", [L, D, 4 * GW])
    w_out = dt_in("w_out", [L, D, D])
    w_gate = dt_in("w_gate", [L, D, DFF])
    w_up = dt_in("w_up", [L, D, DFF])
    w_down = dt_in("w_down", [L, DFF, D])
    w_rg = dt_in("w_rg", [L, 8, 128, 128])
    w_ig = dt_in("w_ig", [L, 8, 128, 128])
    wsT = dt_in("wsT", [L, 128, 8, 128])
    bsp = dt_in("bsp", [L, 1, GW])
    lng = dt_in("lng", [L, GW])
    lnb = dt_in("lnb", [L, GW])
    sci = dt_in("sci", [128, L * 2 * 8 * 3])
    shi = dt_in("shi", [128, L * 2 * 8])
    yT = dt_out("yT", [D, NTOK])
    convP = dt_out("convP", [128, L * 8 * 3])
    lruP = dt_out("lruP", [128, L * 8])
    convS = dt_out("convS", [128, L * 2 * 8 * 3])
    lruS = dt_out("lruS", [128, L * 2 * 8])
    vrows = dt_out("vrows", [L, NS, GW])

    tiles = [(i * T, T, "P") for i in range(NPT)] + [(NPT * T, NS, "S")]
    dt_scr = lambda name, shape: nc.dram_tensor(name, shape, BF16, kind="Internal").ap()
    wb_in = dt_scr("wb_in", [L, D, 4 * GW])
    wb_out = dt_scr("wb_out", [L, D, D])
    wb_gate = dt_scr("wb_gate", [L, D, DFF])
    wb_up = dt_scr("wb_up", [L, D, DFF])
    wb_down = dt_scr("wb_down", [L, DFF, D])

    import contextlib
    with contextlib.ExitStack() as es:
        sb = lambda name, shape, dt=F32: es.enter_context(nc.sbuf_tensor(name, shape, dt))
        ps = lambda name: es.enter_context(nc.psum_tensor(name, [128, 512], F32))
        sem = lambda name: es.enter_context(nc.semaphore(name))
        X = sb("X", [128, 16, T])
        XN = sb("XN", [128, 16, T], BF16)
        OA = sb("OA", [128, 8, T])
        UB = sb("UB", [128, 8, T])
        slots = [sb(f"slot{i}", [128, SLOT], BF16) for i in range(NSLOT)]
        PAR = sb("PAR", [128, NPAR])
        CEXP = sb("CEXP", [128, L * 8])
        TMP8 = sb("TMP8", [128, 4, L * 8])
        WR = sb("WR", [128, L, 8, 128], BF16)
        WI = sb("WI", [128, L, 8, 128], BF16)
        WST = sb("WST", [128, L, 8, 128], BF16)
        WSS = sb("WSS", [128, L, 8, 128], BF16)
        ONES = sb("ONES", [128, 128])
        BROW = sb("BROW", [1, L, GW])
        LNG = sb("LNG", [128, GW])
        LNB = sb("LNB", [128, GW])
        CC = sb("CC", [128, L * 8 * 3])
        HC = sb("HC", [128, L * 8])
        SCI = sb("SCI", [128, L * 2 * 8 * 3])
        SHI = sb("SHI", [128, L * 2 * 8])
        SCO = sb("SCO", [128, L * 2 * 8 * 3])
        SHO = sb("SHO", [128, L * 2 * 8])
        XPs = [sb(f"XP{i}", [128, T + 8]) for i in range(2)]
        XCs = [sb(f"XC{i}", [128, T]) for i in range(2)]
        XCBs = [sb(f"XCB{i}", [128, T], BF16) for i in range(2)]
        RR = sb("RR", [128, T])
        II = sb("II", [128, T])
        AA = sb("AA", [128, T])
        MM = sb("MM", [128, T])
        SQ = [sb(f"SQ{i}", [128, T]) for i in range(2)]
        RST = sb("RST", [128, T])
        RSA = sb("RSA", [128, T])
        RSB = sb("RSB", [128, T])
        VF = sb("VF", [128, GW])
        WSF = VF[:, :].rearrange("p (h t) -> p h t", h=8)
        GG = MM
        VN = sb("VN", [128, 4, GW], BF16)
        BST = sb("BST", [128, 16])
        ACTB = sb("ACTB", [128, 4, T], BF16)
        psA = [ps("psA0"), ps("psA1")]
        psB = [ps("psB0"), ps("psB1")]
        psS = ps("psS")
        psV = [ps("psV0"), ps("psV1")]
        esem = {e: sem("s_" + e) for e in ("pe", "act", "dve", "pool", "sp")}
        slot_sem = [sem(f"sl{i}") for i in range(NSLOT)]
        x_sems = {0: sem("xs0"), 8: sem("xs8")}
        y_sems = {0: sem("ys0"), 8: sem("ys8")}
        c_sem = sem("cs")
        ln_sem = sem("lns")
        o_sem = sem("os")
        v_sem = sem("vs")
        S = Sched(nc, esem)
        csems = []

        def c_new():
            csems.append(sem(f"c{len(csems)}"))
            return csems[-1]
        lng_sem = sem("lng_s")
        lnb_sem = sem("lnb_s")
        nslot = [0]

        def load_w(src_ap, view, skey):
            i = nslot[0] % NSLOT
            nslot[0] += 1
            key = f"slot{i}"
            dst = view(slots[i])
            nk = dst.shape[1]
            step = 8 if nk > 8 else nk
            for k0 in range(0, nk, step):
                S.op("sp", lambda e, d=dst, s=src_ap, k0=k0: e.dma_start(out=d[:, k0:k0 + step, :], in_=s[:, k0:k0 + step, :]),
                     reads=[skey], writes=[key], dma_sem=slot_sem[i])
            return dst, key

        def precast(dst, src, skey, rows, piece):
            sm = sem("pc_" + skey)
            for r0 in range(0, rows, piece):
                S.op("pool", lambda e, r0=r0: e.dma_start(out=dst[r0:r0 + piece, :], in_=src[r0:r0 + piece, :]),
                     writes=[skey], dma_sem=sm)

        v16 = lambda sl: sl[:, :].rearrange("p (k n) -> p k n", k=16)
        v2 = lambda sl: sl[:, :].rearrange("p (k n) -> p k n", k=2)

        S.op("sp", lambda e: e.dma_start(out=PAR[:, :], in_=par[:, :]), writes=["PAR"], dma_sem=c_new())
        S.op("sp", lambda e: e.dma_start(out=SCI[:, :], in_=sci[:, :]), writes=["SCI"], dma_sem=c_new())
        S.op("sp", lambda e: e.dma_start(out=SHI[:, :], in_=shi[:, :]), writes=["SHI"], dma_sem=c_new())
        S.op("sp", lambda e: e.dma_start(out=BROW[:, :, :], in_=bsp.rearrange("l o n -> o l n")),
             writes=["BROW"], dma_sem=c_new())
        S.op("dve", lambda e: e.memset(ONES[:, :], 1.0), writes=["ONES"])
        S.op("dve", lambda e: e.memset(CC[:, :], 0.0), writes=["CC"])
        S.op("dve", lambda e: e.memset(HC[:, :], 0.0), writes=["HC"])
        S.op("dve", lambda e: e.memset(WSF[:, :, :], 0.0), writes=["VF", "VFb"])
        for l in range(L):
            S.op("pool", lambda e, l=l: e.dma_start(out=WR[:, l, :, :], in_=w_rg[l].rearrange("h i j -> i h j")),
                 writes=[f"WR{l}"], dma_sem=c_new())
            S.op("pool", lambda e, l=l: e.dma_start(out=WI[:, l, :, :], in_=w_ig[l].rearrange("h i j -> i h j")),
                 writes=[f"WI{l}"], dma_sem=c_new())
            S.op("pool", lambda e, l=l: e.dma_start(out=WST[:, l, :, :], in_=wsT[l]), writes=[f"WST{l}"], dma_sem=c_new())
            S.op("pool", lambda e, l=l: e.affine_select(
                out=WST[:, l, :, :], in_=WST[:, l, :, :], pattern=[[0, 8], [1, 128]],
                compare_op=ALU.is_ge, fill=0.0, base=0, channel_multiplier=-1),
                reads=[f"WST{l}"], writes=[f"WST{l}"])
            S.op("sp", lambda e, l=l: e.dma_start(out=WSF[0:64, :, 0:64], in_=wsT[l, 0:64, :, 0:64]),
                 writes=["VF"], dma_sem=c_new())
            S.op("sp", lambda e, l=l: e.dma_start(out=WSF[64:128, :, 64:128], in_=wsT[l, 0:64, :, 0:64]),
                 writes=["VFb"], dma_sem=c_new())
            S.op("pool", lambda e, l=l: e.affine_select(
                out=WSS[:, l, :, :], in_=WSF[:, :, :], pattern=[[0, 8], [1, 128]],
                compare_op=ALU.is_ge, fill=0.0, base=0, channel_multiplier=-1),
                reads=["VF", "VFb"], writes=[f"WSS{l}"])
        for l in range(L):
            lam = PAR[:, l * 112 + 88:l * 112 + 96]
            c = slice(l * 8, l * 8 + 8)
            S.op("dve", lambda e, lam=lam, c=c: e.tensor_scalar(
                out=TMP8[:, 0, c], in0=lam, scalar1=-1.0, scalar2=None, op0=ALU.mult),
                reads=["PAR"], writes=["T0"])
            S.op("act", lambda e, c=c: e.activation(out=TMP8[:, 1, c], in_=TMP8[:, 0, c], func=AF.Abs),
                 reads=["T0"], writes=["T1"])
            S.op("act", lambda e, c=c: e.activation(out=TMP8[:, 2, c], in_=TMP8[:, 1, c], func=AF.Exp, scale=-1.0),
                 reads=["T1"], writes=["T2"])
            S.op("act", lambda e, c=c: e.activation(out=TMP8[:, 3, c], in_=TMP8[:, 2, c], func=AF.Ln, bias=1.0),
                 reads=["T2"], writes=["T3"])
            S.op("dve", lambda e, c=c: e.tensor_scalar(
                out=TMP8[:, 0, c], in0=TMP8[:, 0, c], scalar1=0.0, scalar2=None, op0=ALU.max),
                reads=["T0"], writes=["T0"])
            S.op("dve", lambda e, c=c: e.tensor_tensor(out=TMP8[:, 0, c], in0=TMP8[:, 0, c], in1=TMP8[:, 3, c], op=ALU.add),
                 reads=["T0", "T3"], writes=["T0"])
            S.op("dve", lambda e, c=c: e.tensor_scalar(
                out=CEXP[:, c], in0=TMP8[:, 0, c], scalar1=-8.0, scalar2=None, op0=ALU.mult),
                reads=["T0"], writes=["CEXP"])

        for l in range(L):
            precast(wb_in[l], w_in[l], f"wb_in{l}", D, 512)
            precast(wb_out[l], w_out[l], f"wb_out{l}", D, 512)
            precast(wb_gate[l], w_gate[l], f"wb_gate{l}", D, 512)
            precast(wb_up[l], w_up[l], f"wb_up{l}", D, 512)
            precast(wb_down[l], w_down[l], f"wb_down{l}", DFF, 512)

        sqi = [0]

        def rms_stats(src_fn, keys, nchunks, n, dim, out_rstd, out_key):
            for kc in range(nchunks):
                b = sqi[0] % 2
                sqi[0] += 1
                S.op("act", lambda e, kc=kc, b=b: e.activation(out=SQ[b][:, :n], in_=src_fn(kc), func=AF.Square),
                     reads=[keys(kc)], writes=[f"SQ{b}"])
                S.op("pe", lambda e, kc=kc, b=b: e.matmul(psS[:, :n], lhsT=ONES[:, :], rhs=SQ[b][:, :n],
                                                          start=(kc == 0), stop=(kc == nchunks - 1)),
                     reads=[f"SQ{b}", "ONES"], writes=["psS"], signal=True)
            S.op("act", lambda e: e.activation(out=out_rstd[:, :n], in_=psS[:, :n], func=AF.Sqrt,
                                               scale=1.0 / dim, bias=EPS),
                 reads=["psS"], writes=[out_key])
            S.op("dve", lambda e: e.reciprocal(out=out_rstd[:, :n], in_=out_rstd[:, :n]),
                 reads=[out_key], writes=[out_key])

        def norm_to_bf16(n, gcol, rstd, rkey):
            for kc in range(16):
                S.op("dve", lambda e, kc=kc: e.scalar_tensor_tensor(
                    out=XN[:, kc, :n], in0=X[:, kc, :n], scalar=PAR[:, gcol + kc:gcol + kc + 1],
                    in1=rstd[:, :n], op0=ALU.mult, op1=ALU.mult),
                    reads=[f"X{kc}", rkey, "PAR"], writes=[f"XN{kc}"])

        def proj_chunk(pst, pkey, wv, wkey, c0, n):
            for kc in range(16):
                S.op("pe", lambda e, kc=kc: e.matmul(pst[:, :n], lhsT=wv[:, kc, c0:c0 + 128], rhs=XN[:, kc, :n],
                                                     start=(kc == 0), stop=(kc == 15)),
                     reads=[wkey, f"XN{kc}"], writes=[pkey], signal=(kc == 15))

        def layer(l, n, kind, col0):
            pb = l * 112
            segs = [(0, n)] if kind == "P" else [(0, 64), (64, 64)]
            nb = n // 128
            S.op("sp", lambda e: e.dma_start(out=LNG[:, :], in_=lng[l].partition_broadcast(128)),
                 writes=["LNG"], dma_sem=lng_sem)
            S.op("sp", lambda e: e.dma_start(out=LNB[:, :], in_=lnb[l].partition_broadcast(128)),
                 writes=["LNB"], dma_sem=lnb_sem)
            rms_stats(lambda kc: X[:, kc, :n], lambda kc: f"X{kc}", 16, n, D, RST, "RST")
            norm_to_bf16(n, pb + 0, RST, "RST")
            wv_in = wb_in[l].rearrange("(k p) c -> p k c", p=128)
            pi = [0]

            def nextA():
                pi[0] += 1
                return psA[pi[0] % 2], f"psA{pi[0] % 2}"
            xa_w = {}
            ga_w = {}
            u_w = {}

            def stage1(h):
                blk, hh = h // 2, h % 2
                if hh == 0:
                    xa_w[blk] = load_w(wv_in[:, :, blk * 256:(blk + 1) * 256], v16, f"wb_in{l}")
                    ga_w[blk] = load_w(wv_in[:, :, GW + blk * 256:GW + (blk + 1) * 256], v16, f"wb_in{l}")
                    u_w[blk] = load_w(wv_in[:, :, 2 * GW + blk * 256:2 * GW + (blk + 1) * 256], v16, f"wb_in{l}")
                wv, wkey = xa_w[blk]
                bi = h % 2
                XP, XC, XCB = XPs[bi], XCs[bi], XCBs[bi]
                kXP, kXC, kXCB = f"XP{bi}", f"XC{bi}", f"XCB{bi}"
                pst, pkey = nextA()
                proj_chunk(pst, pkey, wv, wkey, hh * 128, n)
                for si, (c0, sn) in enumerate(segs):
                    off = c0 + 3 * si
                    if kind == "P":
                        cin = CC[:, (l * 8 + h) * 3:(l * 8 + h) * 3 + 3]
                        cout = cin
                        cik, cok = "CC", "CC"
                    else:
                        o3 = ((l * 2 + si) * 8 + h) * 3
                        cin, cout = SCI[:, o3:o3 + 3], SCO[:, o3:o3 + 3]
                        cik, cok = "SCI", "SCO"
                    S.op("dve", lambda e, off=off, cin=cin: e.tensor_copy(out=XP[:, off:off + 3], in_=cin),
                         reads=[cik], writes=[kXP])
                    S.op("act", lambda e, off=off, c0=c0, sn=sn, pst=pst: e.activation(
                        out=XP[:, off + 3:off + 3 + sn], in_=pst[:, c0:c0 + sn], func=AF.Copy),
                        reads=[pkey], writes=[kXP])
                    w0 = pb + 32 + 0 * 8 + h
                    S.op("dve", lambda e, off=off, c0=c0, sn=sn, w0=w0, h=h: e.tensor_scalar(
                        out=XC[:, c0:c0 + sn], in0=XP[:, off:off + sn], scalar1=PAR[:, w0:w0 + 1],
                        scalar2=PAR[:, pb + 64 + h:pb + 65 + h], op0=ALU.mult, op1=ALU.add),
                        reads=[kXP, "PAR"], writes=[kXC])
                    for k in range(1, 4):
                        wk = pb + 32 + k * 8 + h
                        S.op("dve", lambda e, off=off, c0=c0, sn=sn, wk=wk, k=k: e.scalar_tensor_tensor(
                            out=XC[:, c0:c0 + sn], in0=XP[:, off + k:off + k + sn], scalar=PAR[:, wk:wk + 1],
                            in1=XC[:, c0:c0 + sn], op0=ALU.mult, op1=ALU.add),
                            reads=[kXP, kXC, "PAR"], writes=[kXC])
                    S.op("dve", lambda e, off=off, sn=sn, cout=cout: e.tensor_copy(
                        out=cout, in_=XP[:, off + sn:off + sn + 3]),
                        reads=[kXP], writes=[cok])
                S.op("act", lambda e: e.activation(out=XCB[:, :n], in_=XC[:, :n], func=AF.Copy),
                     reads=[kXC], writes=[kXCB])

            def stage2(h):
                bi = h % 2
                XC, XCB = XCs[bi], XCBs[bi]
                kXC, kXCB = f"XC{bi}", f"XCB{bi}"
                S.op("pe", lambda e: e.matmul(psB[0][:, :n], lhsT=WR[:, l, h, :], rhs=XCB[:, :n], start=True, stop=True),
                     reads=[kXCB, f"WR{l}"], writes=["psB0"])
                S.op("pe", lambda e: e.matmul(psB[1][:, :n], lhsT=WI[:, l, h, :], rhs=XCB[:, :n], start=True, stop=True),
                     reads=[kXCB, f"WI{l}"], writes=["psB1"])
                S.op("act", lambda e: e.activation(out=RR[:, :n], in_=psB[0][:, :n], func=AF.Sigmoid,
                                                   bias=PAR[:, pb + 72 + h:pb + 73 + h]),
                     reads=["psB0", "PAR"], writes=["RR"])
                S.op("act", lambda e: e.activation(out=II[:, :n], in_=psB[1][:, :n], func=AF.Sigmoid,
                                                   bias=PAR[:, pb + 80 + h:pb + 81 + h]),
                     reads=["psB1", "PAR"], writes=["II"])
                S.op("act", lambda e: e.activation(out=AA[:, :n], in_=RR[:, :n], func=AF.Exp,
                                                   scale=CEXP[:, l * 8 + h:l * 8 + h + 1]),
                     reads=["RR", "CEXP"], writes=["AA"])
                S.op("dve", lambda e: e.tensor_tensor(out=MM[:, :n], in0=AA[:, :n], in1=AA[:, :n], op=ALU.mult),
                     reads=["AA"], writes=["MM"])
                S.op("act", lambda e: e.activation(out=MM[:, :n], in_=MM[:, :n], func=AF.Sqrt, scale=-1.0, bias=1.0),
                     reads=["MM"], writes=["MM"])
                S.op("dve", lambda e: e.tensor_tensor(out=II[:, :n], in0=II[:, :n], in1=XC[:, :n], op=ALU.mult),
                     reads=["II", kXC], writes=["II"])
                S.op("dve", lambda e: e.tensor_tensor(out=II[:, :n], in0=II[:, :n], in1=MM[:, :n], op=ALU.mult),
                     reads=["II", "MM"], writes=["II"])
                for si, (c0, sn) in enumerate(segs):
                    if kind == "P":
                        hin = HC[:, l * 8 + h:l * 8 + h + 1]; hout = hin; hik = hok = "HC"
                    else:
                        o1 = (l * 2 + si) * 8 + h
                        hin, hout = SHI[:, o1:o1 + 1], SHO[:, o1:o1 + 1]; hik, hok = "SHI", "SHO"
                    S.op("dve", lambda e, c0=c0, sn=sn, hin=hin: e.tensor_tensor_scan(
                        out=RR[:, c0:c0 + sn], data0=AA[:, c0:c0 + sn], data1=II[:, c0:c0 + sn],
                        initial=hin, op0=ALU.mult, op1=ALU.add),
                        reads=["AA", "II", hik], writes=["RR"])
                    S.op("dve", lambda e, c0=c0, sn=sn, hout=hout: e.tensor_copy(
                        out=hout, in_=RR[:, c0 + sn - 1:c0 + sn]),
                        reads=["RR"], writes=[hok])
                S.op("dve", lambda e: e.tensor_tensor(out=OA[:, h, :n], in0=OA[:, h, :n], in1=RR[:, :n], op=ALU.mult),
                     reads=["RR", f"OA{h}"], writes=[f"OA{h}"])
                b = sqi[0] % 2
                sqi[0] += 1
                S.op("act", lambda e, b=b: e.activation(out=SQ[b][:, :n], in_=OA[:, h, :n], func=AF.Square),
                     reads=[f"OA{h}"], writes=[f"SQ{b}"])
                S.op("pe", lambda e, b=b: e.matmul(psS[:, :n], lhsT=ONES[:, :], rhs=SQ[b][:, :n],
                                                   start=(h == 0), stop=(h == 7)),
                     reads=[f"SQ{b}", "ONES"], writes=["psS"])

            def side(h):
                blk, hh = h // 2, h % 2
                wv, wkey = ga_w[blk]
                proj_chunk(psV[0], "psV0", wv, wkey, hh * 128, n)
                S.op("act", lambda e: e.activation(out=OA[:, h, :n], in_=psV[0][:, :n], func=AF.Gelu_apprx_tanh),
                     reads=["psV0"], writes=[f"OA{h}"])
                wv, wkey = u_w[blk]
                proj_chunk(psV[1], "psV1", wv, wkey, hh * 128, n)
                S.op("act", lambda e: e.activation(out=UB[:, h, :n], in_=psV[1][:, :n], func=AF.Gelu_apprx_tanh),
                     reads=["psV1"], writes=[f"UB{h}"])

            stage1(0)
            for h in range(8):
                if h + 1 < 8:
                    stage1(h + 1)
                side(h)
                stage2(h)
            S.op("act", lambda e: e.activation(out=RSA[:, :n], in_=psS[:, :n], func=AF.Sqrt, scale=1.0 / GW, bias=EPS),
                 reads=["psS"], writes=["RSA"])
            S.op("dve", lambda e: e.reciprocal(out=RSA[:, :n], in_=RSA[:, :n]), reads=["RSA"], writes=["RSA"])
            vw = [load_w(wv_in[:, :, 3 * GW + blk * 256:3 * GW + (blk + 1) * 256], v16, f"wb_in{l}") for blk in range(4)]
            for tb in range(nb):
                for half in range(2):
                    for q in range(2):
                        wv, wkey = vw[half * 2 + q]
                        for kc in range(16):
                            S.op("pe", lambda e, kc=kc, wv=wv, half=half, q=q, tb=tb: e.matmul(
                                psV[half][:, q * 256:(q + 1) * 256], lhsT=XN[:, kc, tb * 128:(tb + 1) * 128],
                                rhs=wv[:, kc, :], start=(kc == 0), stop=(kc == 15)),
                                reads=[wkey, f"XN{kc}"], writes=[f"psV{half}"], signal=(kc == 15))
                    S.op("act", lambda e, half=half: e.activation(
                        out=VF[:, half * 512:(half + 1) * 512], in_=psV[half][:, :], func=AF.Gelu_apprx_tanh),
                        reads=[f"psV{half}"], writes=["VF"])
                    S.op("dve", lambda e, half=half: e.bn_stats(out=BST[:, half * 6:half * 6 + 6],
                                                                in_=VF[:, half * 512:(half + 1) * 512]),
                         reads=["VF"], writes=["BST"])
                S.op("dve", lambda e: e.bn_aggr(out=BST[:, 12:14], in_=BST[:, 0:12]), reads=["BST"], writes=["BST"])
                S.op("act", lambda e: e.activation(out=BST[:, 14:15], in_=BST[:, 13:14], func=AF.Sqrt, bias=EPS),
                     reads=["BST"], writes=["BST"])
                S.op("dve", lambda e: e.reciprocal(out=BST[:, 14:15], in_=BST[:, 14:15]), reads=["BST"], writes=["BST"])
                S.op("dve", lambda e: e.tensor_scalar(out=VF[:, :], in0=VF[:, :], scalar1=BST[:, 12:13],
                                                      scalar2=BST[:, 14:15], op0=ALU.subtract, op1=ALU.mult),
                     reads=["VF", "BST"], writes=["VF"])
                S.op("dve", lambda e: e.tensor_tensor(out=VF[:, :], in0=VF[:, :], in1=LNG[:, :], op=ALU.mult),
                     reads=["VF", "LNG"], writes=["VF"])
                S.op("dve", lambda e: e.tensor_tensor(out=VF[:, :], in0=VF[:, :], in1=LNB[:, :], op=ALU.add),
                     reads=["VF", "LNB"], writes=["VF"])
                S.op("act", lambda e, tb=tb: e.activation(out=VN[:, tb, :], in_=VF[:, :], func=AF.Copy),
                     reads=["VF"], writes=["VN"])
                if kind == "S":
                    S.op("sp", lambda e: e.dma_start(out=vrows[l], in_=VF[:, :]), reads=["VF"], writes=["vrows"],
                         dma_sem=v_sem)
            wmix = WST if kind == "P" else WSS
            for h in range(8):
                pst, pkey = nextA()
                for tb in range(nb):
                    S.op("pe", lambda e, h=h, tb=tb, pst=pst: e.matmul(
                        pst[:, tb * 128:(tb + 1) * 128], lhsT=VN[:, tb, h * 128:(h + 1) * 128], rhs=wmix[:, l, h, :],
                        start=True, stop=False),
                        reads=["VN", f"WST{l}", f"WSS{l}"], writes=[pkey], signal=False)
                    if kind == "P":
                        S.op("pe", lambda e, h=h, tb=tb, pst=pst: e.matmul(
                            pst[:, tb * 128:(tb + 1) * 128], lhsT=ONES[0:1, :],
                            rhs=BROW[0:1, l, h * 128:h * 128 + 128], start=False, stop=True),
                            reads=["BROW", "ONES"], writes=[pkey], signal=(tb == nb - 1))
                    else:
                        for half in range(2):
                            S.op("pe", lambda e, h=h, half=half, pst=pst: e.matmul(
                                pst[:, half * 64:half * 64 + 64], lhsT=ONES[0:1, :],
                                rhs=BROW[0:1, l, h * 128:h * 128 + 64], start=False, stop=True),
                                reads=["BROW", "ONES"], writes=[pkey], signal=(half == 1))
                S.op("dve", lambda e, h=h, pst=pst: e.tensor_tensor(out=UB[:, h, :n], in0=pst[:, :n], in1=UB[:, h, :n], op=ALU.mult),
                     reads=[pkey, f"UB{h}"], writes=[f"UB{h}"])
                b = sqi[0] % 2
                sqi[0] += 1
                S.op("act", lambda e, h=h, b=b: e.activation(out=SQ[b][:, :n], in_=UB[:, h, :n], func=AF.Square),
                     reads=[f"UB{h}"], writes=[f"SQ{b}"])
                S.op("pe", lambda e, h=h, b=b: e.matmul(psS[:, :n], lhsT=ONES[:, :], rhs=SQ[b][:, :n],
                                                        start=(h == 0), stop=(h == 7)),
                     reads=[f"SQ{b}", "ONES"], writes=["psS"])
            S.op("act", lambda e: e.activation(out=RSB[:, :n], in_=psS[:, :n], func=AF.Sqrt, scale=1.0 / GW, bias=EPS),
                 reads=["psS"], writes=["RSB"])
            S.op("dve", lambda e: e.reciprocal(out=RSB[:, :n], in_=RSB[:, :n]), reads=["RSB"], writes=["RSB"])
            for h in range(8):
                S.op("dve", lambda e, h=h: e.scalar_tensor_tensor(
                    out=XN[:, h, :n], in0=OA[:, h, :n], scalar=PAR[:, pb + 96 + h:pb + 97 + h], in1=RSA[:, :n],
                    op0=ALU.mult, op1=ALU.mult), reads=[f"OA{h}", "RSA", "PAR"], writes=[f"XN{h}"])
                S.op("dve", lambda e, h=h: e.scalar_tensor_tensor(
                    out=XN[:, 8 + h, :n], in0=UB[:, h, :n], scalar=PAR[:, pb + 104 + h:pb + 105 + h], in1=RSB[:, :n],
                    op0=ALU.mult, op1=ALU.mult), reads=[f"UB{h}", "RSB", "PAR"], writes=[f"XN{8 + h}"])
            wv_o = wb_out[l].rearrange("(k p) c -> p k c", p=128)
            for blk in range(8):
                wv, wkey = load_w(wv_o[:, :, blk * 256:(blk + 1) * 256], v16, f"wb_out{l}")
                for hh in range(2):
                    oc = blk * 2 + hh
                    pst, pkey = nextA()
                    proj_chunk(pst, pkey, wv, wkey, hh * 128, n)
                    S.op("dve", lambda e, oc=oc, pst=pst: e.tensor_tensor(out=X[:, oc, :n], in0=pst[:, :n], in1=X[:, oc, :n], op=ALU.add),
                         reads=[pkey, f"X{oc}"], writes=[f"X{oc}"])
            rms_stats(lambda kc: X[:, kc, :n], lambda kc: f"X{kc}", 16, n, D, RST, "RST")
            norm_to_bf16(n, pb + 16, RST, "RST")
            wv_g = wb_gate[l].rearrange("(k p) c -> p k c", p=128)
            wv_u = wb_up[l].rearrange("(k p) c -> p k c", p=128)
            wv_d = wb_down[l].rearrange("(k p) c -> p k c", p=128)
            NST = DFF // 256
            dslots = {}

            def ffn_gu(st):
                gv, gk = load_w(wv_g[:, :, st * 256:(st + 1) * 256], v16, f"wb_gate{l}")
                uv, uk = load_w(wv_u[:, :, st * 256:(st + 1) * 256], v16, f"wb_up{l}")
                dslots[st] = load_w(wv_d[:, st * 2:st * 2 + 2, :], v2, f"wb_down{l}")
                for j in range(2):
                    a = (st % 2) * 2 + j
                    proj_chunk(psA[j], f"psA{j}", gv, gk, j * 128, n)
                    proj_chunk(psB[j], f"psB{j}", uv, uk, j * 128, n)
                    S.op("act", lambda e, j=j: e.activation(out=GG[:, :n], in_=psA[j][:, :n], func=AF.Silu),
                         reads=[f"psA{j}"], writes=["MM"])
                    S.op("dve", lambda e, j=j, a=a: e.tensor_tensor(out=ACTB[:, a, :n], in0=psB[j][:, :n], in1=GG[:, :n], op=ALU.mult),
                         reads=[f"psB{j}", "MM"], writes=[f"ACTB{a}"])

            def ffn_d_pair(p):
                d0, k0 = dslots.pop(2 * p)
                d1, k1 = dslots.pop(2 * p + 1)
                srcs = [(d0, k0, 0, 0), (d0, k0, 1, 1), (d1, k1, 0, 2), (d1, k1, 1, 3)]
                for oc in range(16):
                    pv = psV[oc % 2]
                    pk = f"psV{oc % 2}"
                    for i, (dv, dk, j, a) in enumerate(srcs):
                        S.op("pe", lambda e, oc=oc, j=j, pv=pv, a=a, dv=dv, i=i: e.matmul(
                            pv[:, :n], lhsT=dv[:, j, oc * 128:(oc + 1) * 128], rhs=ACTB[:, a, :n],
                            start=(i == 0), stop=(i == 3)),
                            reads=[dk, f"ACTB{a}"], writes=[pk], signal=(i == 3))
                    S.op("dve", lambda e, oc=oc, pv=pv: e.tensor_tensor(out=X[:, oc, :n], in0=pv[:, :n], in1=X[:, oc, :n], op=ALU.add),
                         reads=[pk, f"X{oc}"], writes=[f"X{oc}"])

            assert NST % 2 == 0
            for st in range(NST):
                ffn_gu(st)
                if st % 2 == 1:
                    ffn_d_pair(st // 2)

        xTv = xT.rearrange("(k p) t -> p k t", p=128)
        yTv = yT.rearrange("(k p) t -> p k t", p=128)
        for (col0, n, kind) in tiles:
            for k0 in (0, 8):
                S.op("sp", lambda e, col0=col0, n=n, k0=k0: e.dma_start(
                    out=X[:, k0:k0 + 8, :n], in_=xTv[:, k0:k0 + 8, col0:col0 + n]),
                    writes=[f"X{kc}" for kc in range(k0, k0 + 8)], dma_sem=x_sems[k0])
            for l in range(L):
                layer(l, n, kind, col0)
            rms_stats(lambda kc, n=n: X[:, kc, :n], lambda kc: f"X{kc}", 16, n, D, RST, "RST")
            for kc in range(16):
                S.op("dve", lambda e, kc=kc, n=n: e.scalar_tensor_tensor(
                    out=X[:, kc, :n], in0=X[:, kc, :n], scalar=PAR[:, 224 + kc:225 + kc], in1=RST[:, :n],
                    op0=ALU.mult, op1=ALU.mult), reads=[f"X{kc}", "RST", "PAR"], writes=[f"X{kc}"])
            for k0 in (0, 8):
                S.op("sp", lambda e, col0=col0, n=n, k0=k0: e.dma_start(
                    out=yTv[:, k0:k0 + 8, col0:col0 + n], in_=X[:, k0:k0 + 8, :n]),
                    reads=[f"X{kc}" for kc in range(k0, k0 + 8)], writes=[f"yT{k0}"], dma_sem=y_sems[k0])
        S.op("sp", lambda e: e.dma_start(out=convP[:, :], in_=CC[:, :]), reads=["CC"], writes=["o1"], dma_sem=o_sem)
        S.op("sp", lambda e: e.dma_start(out=lruP[:, :], in_=HC[:, :]), reads=["HC"], writes=["o2"], dma_sem=o_sem)
        S.op("sp", lambda e: e.dma_start(out=convS[:, :], in_=SCO[:, :]), reads=["SCO"], writes=["o3"], dma_sem=o_sem)
        S.op("sp", lambda e: e.dma_start(out=lruS[:, :], in_=SHO[:, :]), reads=["SHO"], writes=["o4"], dma_sem=o_sem)
        n_y = len(tiles) * 16
        n_o = 4 * 16
        n_v = L * 16

        with nc.Block() as block:
            @block.sync
            def _(e):
                S.emit("sp", e)
                e.wait_ge(y_sems[0], n_y)
                e.wait_ge(y_sems[8], n_y)
                e.wait_ge(o_sem, n_o)
                e.wait_ge(v_sem, n_v)

            @block.gpsimd
            def _(e):
                S.emit("pool", e)

            @block.scalar
            def _(e):
                S.emit("act", e)

            @block.vector
            def _(e):
                S.emit("dve", e)

            @block.tensor
            def _(e):
                S.emit("pe", e)
    return nc


def _vec_pk(v, k):
    return np.ascontiguousarray(v.reshape(k, 128).T)


def run(NPT, x_prompt, x_sample, state_conv, state_lru, norm1, w_in, conv_w, conv_b,
        w_rgate, b_rgate, w_igate, b_igate, lru_param, v_ln_g, v_ln_b,
        w_spatial, b_spatial, gn_a, gn_b, w_out, norm2, w_gate, w_up, w_down, norm_f):
    f = lambda a: np.ascontiguousarray(np.asarray(a, dtype=np.float32))
    x_prompt, x_sample, state_conv, state_lru = map(f, (x_prompt, x_sample, state_conv, state_lru))
    B, SEQ, _ = x_prompt.shape
    assert SEQ == NPT * T
    NTOK = SEQ + NS
    par = np.zeros((128, NPAR), np.float32)
    for l in range(L):
        pb = l * 112
        par[:, pb:pb + 16] = _vec_pk(f(norm1[l]), 16)
        par[:, pb + 16:pb + 32] = _vec_pk(f(norm2[l]), 16)
        for k in range(4):
            par[:, pb + 32 + k * 8:pb + 40 + k * 8] = _vec_pk(f(conv_w[l, k]), 8)
        par[:, pb + 64:pb + 72] = _vec_pk(f(conv_b[l]), 8)
        par[:, pb + 72:pb + 80] = _vec_pk(f(b_rgate[l]), 8)
        par[:, pb + 80:pb + 88] = _vec_pk(f(b_igate[l]), 8)
        par[:, pb + 88:pb + 96] = _vec_pk(f(lru_param[l]), 8)
        par[:, pb + 96:pb + 104] = _vec_pk(f(gn_a[l]), 8)
        par[:, pb + 104:pb + 112] = _vec_pk(f(gn_b[l]), 8)
    par[:, 224:240] = _vec_pk(f(norm_f), 16)
    wsT = np.ascontiguousarray(f(w_spatial).transpose(0, 3, 1, 2))
    bsp = np.ascontiguousarray(f(b_spatial).reshape(L, 1, GW))
    shared = dict(par=par, w_in=f(w_in), w_out=f(w_out), w_gate=f(w_gate), w_up=f(w_up), w_down=f(w_down),
                  w_rg=f(w_rgate), w_ig=f(w_igate), wsT=wsT, bsp=bsp, lng=f(v_ln_g), lnb=f(v_ln_b))
    in_maps = []
    for c in range(8):
        xT = np.zeros((D, NTOK), np.float32)
        if c < B:
            xT[:, :SEQ] = x_prompt[c].T
        xs = x_sample[2 * c:2 * c + 2].reshape(NS, D)
        xT[:, SEQ:] = xs.T
        sc = state_conv[:, 2 * c:2 * c + 2]
        sci = sc.reshape(L, 2, 3, 8, 128).transpose(4, 0, 1, 3, 2).reshape(128, L * 2 * 8 * 3)
        sh = state_lru[:, 2 * c:2 * c + 2]
        shi = sh.reshape(L, 2, 8, 128).transpose(3, 0, 1, 2).reshape(128, L * 2 * 8)
        m = dict(shared)
        m.update(xT=xT, sci=np.ascontiguousarray(sci), shi=np.ascontiguousarray(shi))
        in_maps.append(m)
    nc = build_nc(NPT)
    res = run_bass_kernel_spmd(nc, in_maps, core_ids=list(range(8)))
    R = res.results
    y_prompt = np.stack([R[c]["yT"][:, :SEQ].T for c in range(B)]).astype(np.float32)
    y_sample = np.concatenate([R[c]["yT"][:, SEQ:].T.reshape(2, 64, D) for c in range(8)]).astype(np.float32)
    ncp = np.stack([R[c]["convP"].reshape(128, L, 8, 3).transpose(1, 3, 2, 0).reshape(L, 3, GW) for c in range(B)], axis=1)
    nlp = np.stack([R[c]["lruP"].reshape(128, L, 8).transpose(1, 2, 0).reshape(L, GW) for c in range(B)], axis=1)
    ncs = np.concatenate([R[c]["convS"].reshape(128, L, 2, 8, 3).transpose(1, 2, 4, 3, 0).reshape(L, 2, 3, GW)
                          for c in range(8)], axis=1)
    nls = np.concatenate([R[c]["lruS"].reshape(128, L, 2, 8).transpose(1, 2, 3, 0).reshape(L, 2, GW)
                          for c in range(8)], axis=1)
    nvs = np.concatenate([R[c]["vrows"].reshape(L, 2, 64, GW) for c in range(8)], axis=1)
    out = (y_prompt, y_sample, ncp, nlp, ncs, nls, nvs)
    return tuple(np.ascontiguousarray(o, dtype=np.float32) for o in out)


def kernel(**inputs):
    return run(16, **inputs)
```

```python
import numpy as np
import concourse.bass as bass
import concourse.mybir as mybir
from concourse.bass_utils import run_bass_kernel_spmd

F32 = mybir.dt.float32
BF16 = mybir.dt.bfloat16
AF = mybir.ActivationFunctionType
ALU = mybir.AluOpType

D = 2048
GW = 1024
DFF = 5632
L = 2
T = 512
NS = 128
EPS = 1e-6
NPAR = 240
SLOT = 4096
NSLOT = 6


class Sched:
    def __init__(self, nc, esem):
        self.nc = nc
        self.esem = esem
        self.ops = {e: [] for e in esem}
        self.cnt = {e: 0 for e in esem}
        self.prog = {e: [] for e in esem}
        self.lastw = {}
        self.readers = {}
        self.waited = {e: {} for e in esem}
        self.dcnt = {}

    def _resolve(self, ref, eng):
        if ref[0] == "dma":
            return (ref[1], ref[2])
        e, idx = ref
        if e == eng and e == "pe":
            return None
        lst = self.ops[e]
        for j in range(idx, len(lst)):
            if lst[j] is not None:
                return (self.esem[e], lst[j])
        raise RuntimeError(f"unsignaled dep on {e} idx {idx}")

    def op(self, eng, fn, reads=(), writes=(), signal=True, dma_sem=None):
        deps = []
        for k in reads:
            if k in self.lastw:
                deps.append(self.lastw[k])
        for k in writes:
            if k in self.lastw:
                deps.append(self.lastw[k])
            deps.extend(self.readers.get(k, ()))
        waits = {}
        for d in deps:
            r = self._resolve(d, eng)
            if r is None:
                continue
            s, v = r
            key = id(s)
            if self.waited[eng].get(key, 0) >= v:
                continue
            if key not in waits or waits[key][1] < v:
                waits[key] = (s, v)
        for key, (s, v) in waits.items():
            self.waited[eng][key] = v
        if dma_sem is not None:
            self.dcnt[id(dma_sem)] = self.dcnt.get(id(dma_sem), 0) + 16
            ref = ("dma", dma_sem, self.dcnt[id(dma_sem)])
            self.ops[eng].append(None)
            inc = (dma_sem, 16)
        else:
            if signal:
                self.cnt[eng] += 1
                self.ops[eng].append(self.cnt[eng])
                inc = (self.esem[eng], 1)
            else:
                self.ops[eng].append(None)
                inc = None
            ref = (eng, len(self.ops[eng]) - 1)
        self.prog[eng].append((list(waits.values()), fn, inc))
        for k in writes:
            self.lastw[k] = ref
            self.readers[k] = []
        for k in reads:
            if k not in writes:
                self.readers.setdefault(k, []).append(ref)

    def emit(self, eng, e):
        for waits, fn, inc in self.prog[eng]:
            for s, v in waits:
                e.wait_ge(s, v)
            ins = fn(e)
            if inc is not None:
                ins.then_inc(inc[0], inc[1])


def build_nc(NPT):
    NTOK = NPT * T + NS
    nc = bass.Bass("TRN2", target_bir_lowering=False)
    dt_in = lambda name, shape: nc.dram_tensor(name, shape, F32, kind="ExternalInput").ap()
    dt_out = lambda name, shape: nc.dram_tensor(name, shape, F32, kind="ExternalOutput").ap()
    xT = dt_in("xT", [D, NTOK])
    par = dt_in("par", [128, NPAR])
    w_in = dt_in("w_in", [L, D, 4 * GW])
    w_out = dt_in("w_out", [L, D, D])
    w_gate = dt_in("w_gate", [L, D, DFF])
    w_up = dt_in("w_up", [L, D, DFF])
    w_down = dt_in("w_down", [L, DFF, D])
    w_rg = dt_in("w_rg", [L, 8, 128, 128])
    w_ig = dt_in("w_ig", [L, 8, 128, 128])
    wsT = dt_in("wsT", [L, 128, 8, 128])
    bsp = dt_in("bsp", [L, 1, GW])
    lng = dt_in("lng", [L, GW])
    lnb = dt_in("lnb", [L, GW])
    sci = dt_in("sci", [128, L * 2 * 8 * 3])
    shi = dt_in("shi", [128, L * 2 * 8])
    OWN = NPT // 4
    NPRE = NPT - OWN
    flg = dt_in("flg", [128, NPT])
    yT = dt_out("yT", [D, OWN * T + NS])
    convP = dt_out("convP", [128, L * 8 * 3])
    lruP = dt_out("lruP", [128, L * 8])
    convS = dt_out("convS", [128, L * 2 * 8 * 3])
    lruS = dt_out("lruS", [128, L * 2 * 8])
    vrows = dt_out("vrows", [L, NS, GW])

    tiles = [(i * T, T, "P") for i in range(NPT)] + [(NPT * T, NS, "S")]
    dt_scr = lambda name, shape: nc.dram_tensor(name, shape, BF16, kind="Internal").ap()
    wb_in = dt_scr("wb_in", [L, D, 4 * GW])
    wb_out = dt_scr("wb_out", [L, D, D])
    wb_gate = dt_scr("wb_gate", [L, D, DFF])
    wb_up = dt_scr("wb_up", [L, D, DFF])
    wb_down = dt_scr("wb_down", [L, DFF, D])

    import contextlib
    with contextlib.ExitStack() as es:
        sb = lambda name, shape, dt=F32: es.enter_context(nc.sbuf_tensor(name, shape, dt))
        ps = lambda name: es.enter_context(nc.psum_tensor(name, [128, 512], F32))
        sem = lambda name: es.enter_context(nc.semaphore(name))
        X = sb("X", [128, 16, T])
        XN = sb("XN", [128, 16, T], BF16)
        OA = sb("OA", [128, 8, T])
        UB = sb("UB", [128, 8, T])
        slots = [sb(f"slot{i}", [128, SLOT], BF16) for i in range(NSLOT)]
        PAR = sb("PAR", [128, NPAR])
        CEXP = sb("CEXP", [128, L * 8])
        TMP8 = sb("TMP8", [128, 4, L * 8])
        WR = sb("WR", [128, L, 8, 128], BF16)
        WI = sb("WI", [128, L, 8, 128], BF16)
        WST = sb("WST", [128, L, 8, 128], BF16)
        WSS = sb("WSS", [128, L, 8, 128], BF16)
        ONES = sb("ONES", [128, 128])
        BROW = sb("BROW", [1, L, GW])
        LNG = sb("LNG", [128, GW])
        LNB = sb("LNB", [128, GW])
        CC = sb("CC", [128, L * 8 * 3])
        HC = sb("HC", [128, L * 8])
        SCI = sb("SCI", [128, L * 2 * 8 * 3])
        SHI = sb("SHI", [128, L * 2 * 8])
        SCO = sb("SCO", [128, L * 2 * 8 * 3])
        SHO = sb("SHO", [128, L * 2 * 8])
        XPs = [sb(f"XP{i}", [128, T + 8]) for i in range(2)]
        XCs = [sb(f"XC{i}", [128, T]) for i in range(2)]
        XCBs = [sb(f"XCB{i}", [128, T], BF16) for i in range(2)]
        RR = sb("RR", [128, T])
        II = sb("II", [128, T])
        AA = sb("AA", [128, T])
        MM = sb("MM", [128, T])
        SQ = [sb(f"SQ{i}", [128, T]) for i in range(2)]
        RST = sb("RST", [128, T])
        RSA = sb("RSA", [128, T])
        RSB = sb("RSB", [128, T])
        VF = sb("VF", [128, GW])
        WSF = VF[:, :].rearrange("p (h t) -> p h t", h=8)
        GG = MM
        VN = sb("VN", [128, 4, GW], BF16)
        BST = sb("BST", [128, 16])
        ACTB = sb("ACTB", [128, 4, T], BF16)
        FLG = sb("FLG", [128, NPT])
        psA = [ps("psA0"), ps("psA1")]
        psB = [ps("psB0"), ps("psB1")]
        psS = ps("psS")
        psV = [ps("psV0"), ps("psV1")]
        esem = {e: sem("s_" + e) for e in ("pe", "act", "dve", "pool", "sp")}
        slot_sem = [sem(f"sl{i}") for i in range(NSLOT)]
        x_sems = {0: sem("xs0"), 8: sem("xs8")}
        y_sems = {0: sem("ys0"), 8: sem("ys8")}
        c_sem = sem("cs")
        ln_sem = sem("lns")
        o_sem = sem("os")
        v_sem = sem("vs")
        S = Sched(nc, esem)
        csems = []

        def c_new():
            csems.append(sem(f"c{len(csems)}"))
            return csems[-1]
        lng_sem = sem("lng_s")
        lnb_sem = sem("lnb_s")
        nslot = [0]

        def load_w(src_ap, view, skey):
            i = nslot[0] % NSLOT
            nslot[0] += 1
            key = f"slot{i}"
            dst = view(slots[i])
            nk = dst.shape[1]
            step = 8 if nk > 8 else nk
            for k0 in range(0, nk, step):
                S.op("sp", lambda e, d=dst, s=src_ap, k0=k0: e.dma_start(out=d[:, k0:k0 + step, :], in_=s[:, k0:k0 + step, :]),
                     reads=[skey], writes=[key], dma_sem=slot_sem[i])
            return dst, key

        def precast(dst, src, skey, rows, piece):
            sm = sem("pc_" + skey)
            for r0 in range(0, rows, piece):
                S.op("pool", lambda e, r0=r0: e.dma_start(out=dst[r0:r0 + piece, :], in_=src[r0:r0 + piece, :]),
                     writes=[skey], dma_sem=sm)

        v16 = lambda sl: sl[:, :].rearrange("p (k n) -> p k n", k=16)
        v2 = lambda sl: sl[:, :].rearrange("p (k n) -> p k n", k=2)

        S.op("sp", lambda e: e.dma_start(out=PAR[:, :], in_=par[:, :]), writes=["PAR"], dma_sem=c_new())
        S.op("sp", lambda e: e.dma_start(out=SCI[:, :], in_=sci[:, :]), writes=["SCI"], dma_sem=c_new())
        S.op("sp", lambda e: e.dma_start(out=SHI[:, :], in_=shi[:, :]), writes=["SHI"], dma_sem=c_new())
        S.op("sp", lambda e: e.dma_start(out=BROW[:, :, :], in_=bsp.rearrange("l o n -> o l n")),
             writes=["BROW"], dma_sem=c_new())
        S.op("sp", lambda e: e.dma_start(out=FLG[:, :], in_=flg[:, :]), writes=["FLG"], dma_sem=c_new())
        S.op("dve", lambda e: e.memset(ONES[:, :], 1.0), writes=["ONES"])
        S.op("dve", lambda e: e.memset(CC[:, :], 0.0), writes=["CC"])
        S.op("dve", lambda e: e.memset(HC[:, :], 0.0), writes=["HC"])
        S.op("dve", lambda e: e.memset(WSF[:, :, :], 0.0), writes=["VF", "VFb"])
        for l in range(L):
            S.op("pool", lambda e, l=l: e.dma_start(out=WR[:, l, :, :], in_=w_rg[l].rearrange("h i j -> i h j")),
                 writes=[f"WR{l}"], dma_sem=c_new())
            S.op("pool", lambda e, l=l: e.dma_start(out=WI[:, l, :, :], in_=w_ig[l].rearrange("h i j -> i h j")),
                 writes=[f"WI{l}"], dma_sem=c_new())
            S.op("pool", lambda e, l=l: e.dma_start(out=WST[:, l, :, :], in_=wsT[l]), writes=[f"WST{l}"], dma_sem=c_new())
            S.op("pool", lambda e, l=l: e.affine_select(
                out=WST[:, l, :, :], in_=WST[:, l, :, :], pattern=[[0, 8], [1, 128]],
                compare_op=ALU.is_ge, fill=0.0, base=0, channel_multiplier=-1),
                reads=[f"WST{l}"], writes=[f"WST{l}"])
            S.op("sp", lambda e, l=l: e.dma_start(out=WSF[0:64, :, 0:64], in_=wsT[l, 0:64, :, 0:64]),
                 writes=["VF"], dma_sem=c_new())
            S.op("sp", lambda e, l=l: e.dma_start(out=WSF[64:128, :, 64:128], in_=wsT[l, 0:64, :, 0:64]),
                 writes=["VFb"], dma_sem=c_new())
            S.op("pool", lambda e, l=l: e.affine_select(
                out=WSS[:, l, :, :], in_=WSF[:, :, :], pattern=[[0, 8], [1, 128]],
                compare_op=ALU.is_ge, fill=0.0, base=0, channel_multiplier=-1),
                reads=["VF", "VFb"], writes=[f"WSS{l}"])
        for l in range(L):
            lam = PAR[:, l * 112 + 88:l * 112 + 96]
            c = slice(l * 8, l * 8 + 8)
            S.op("dve", lambda e, lam=lam, c=c: e.tensor_scalar(
                out=TMP8[:, 0, c], in0=lam, scalar1=-1.0, scalar2=None, op0=ALU.mult),
                reads=["PAR"], writes=["T0"])
            S.op("act", lambda e, c=c: e.activation(out=TMP8[:, 1, c], in_=TMP8[:, 0, c], func=AF.Abs),
                 reads=["T0"], writes=["T1"])
            S.op("act", lambda e, c=c: e.activation(out=TMP8[:, 2, c], in_=TMP8[:, 1, c], func=AF.Exp, scale=-1.0),
                 reads=["T1"], writes=["T2"])
            S.op("act", lambda e, c=c: e.activation(out=TMP8[:, 3, c], in_=TMP8[:, 2, c], func=AF.Ln, bias=1.0),
                 reads=["T2"], writes=["T3"])
            S.op("dve", lambda e, c=c: e.tensor_scalar(
                out=TMP8[:, 0, c], in0=TMP8[:, 0, c], scalar1=0.0, scalar2=None, op0=ALU.max),
                reads=["T0"], writes=["T0"])
            S.op("dve", lambda e, c=c: e.tensor_tensor(out=TMP8[:, 0, c], in0=TMP8[:, 0, c], in1=TMP8[:, 3, c], op=ALU.add),
                 reads=["T0", "T3"], writes=["T0"])
            S.op("dve", lambda e, c=c: e.tensor_scalar(
                out=CEXP[:, c], in0=TMP8[:, 0, c], scalar1=-8.0, scalar2=None, op0=ALU.mult),
                reads=["T0"], writes=["CEXP"])

        for l in range(L):
            precast(wb_in[l], w_in[l], f"wb_in{l}", D, 512)
            precast(wb_out[l], w_out[l], f"wb_out{l}", D, 512)
            precast(wb_gate[l], w_gate[l], f"wb_gate{l}", D, 512)
            precast(wb_up[l], w_up[l], f"wb_up{l}", D, 512)
            precast(wb_down[l], w_down[l], f"wb_down{l}", DFF, 512)

        sqi = [0]

        def rms_stats(src_fn, keys, nchunks, n, dim, out_rstd, out_key):
            for kc in range(nchunks):
                b = sqi[0] % 2
                sqi[0] += 1
                S.op("act", lambda e, kc=kc, b=b: e.activation(out=SQ[b][:, :n], in_=src_fn(kc), func=AF.Square),
                     reads=[keys(kc)], writes=[f"SQ{b}"])
                S.op("pe", lambda e, kc=kc, b=b: e.matmul(psS[:, :n], lhsT=ONES[:, :], rhs=SQ[b][:, :n],
                                                          start=(kc == 0), stop=(kc == nchunks - 1)),
                     reads=[f"SQ{b}", "ONES"], writes=["psS"], signal=True)
            S.op("act", lambda e: e.activation(out=out_rstd[:, :n], in_=psS[:, :n], func=AF.Sqrt,
                                               scale=1.0 / dim, bias=EPS),
                 reads=["psS"], writes=[out_key])
            S.op("dve", lambda e: e.reciprocal(out=out_rstd[:, :n], in_=out_rstd[:, :n]),
                 reads=[out_key], writes=[out_key])

        def norm_to_bf16(n, gcol, rstd, rkey):
            for kc in range(16):
                S.op("dve", lambda e, kc=kc: e.scalar_tensor_tensor(
                    out=XN[:, kc, :n], in0=X[:, kc, :n], scalar=PAR[:, gcol + kc:gcol + kc + 1],
                    in1=rstd[:, :n], op0=ALU.mult, op1=ALU.mult),
                    reads=[f"X{kc}", rkey, "PAR"], writes=[f"XN{kc}"])

        def proj_chunk(pst, pkey, wv, wkey, c0, n):
            for kc in range(16):
                S.op("pe", lambda e, kc=kc: e.matmul(pst[:, :n], lhsT=wv[:, kc, c0:c0 + 128], rhs=XN[:, kc, :n],
                                                     start=(kc == 0), stop=(kc == 15)),
                     reads=[wkey, f"XN{kc}"], writes=[pkey], signal=(kc == 15))

        def layer(l, n, kind, col0, mode="full"):
            full = (mode == "full")
            pb = l * 112
            segs = [(0, n)] if kind == "P" else [(0, 64), (64, 64)]
            nb = n // 128
            if full:
                S.op("sp", lambda e: e.dma_start(out=LNG[:, :], in_=lng[l].partition_broadcast(128)),
                     writes=["LNG"], dma_sem=lng_sem)
                S.op("sp", lambda e: e.dma_start(out=LNB[:, :], in_=lnb[l].partition_broadcast(128)),
                     writes=["LNB"], dma_sem=lnb_sem)
            rms_stats(lambda kc: X[:, kc, :n], lambda kc: f"X{kc}", 16, n, D, RST, "RST")
            norm_to_bf16(n, pb + 0, RST, "RST")
            wv_in = wb_in[l].rearrange("(k p) c -> p k c", p=128)
            pi = [0]

            def nextA():
                pi[0] += 1
                return psA[pi[0] % 2], f"psA{pi[0] % 2}"
            xa_w = {}
            ga_w = {}
            u_w = {}

            def stage1(h):
                blk, hh = h // 2, h % 2
                if hh == 0:
                    xa_w[blk] = load_w(wv_in[:, :, blk * 256:(blk + 1) * 256], v16, f"wb_in{l}")
                    if full:
                        ga_w[blk] = load_w(wv_in[:, :, GW + blk * 256:GW + (blk + 1) * 256], v16, f"wb_in{l}")
                        u_w[blk] = load_w(wv_in[:, :, 2 * GW + blk * 256:2 * GW + (blk + 1) * 256], v16, f"wb_in{l}")
                wv, wkey = xa_w[blk]
                bi = h % 2
                XP, XC, XCB = XPs[bi], XCs[bi], XCBs[bi]
                kXP, kXC, kXCB = f"XP{bi}", f"XC{bi}", f"XCB{bi}"
                pst, pkey = nextA()
                proj_chunk(pst, pkey, wv, wkey, hh * 128, n)
                for si, (c0, sn) in enumerate(segs):
                    off = c0 + 3 * si
                    if kind == "P":
                        cin = CC[:, (l * 8 + h) * 3:(l * 8 + h) * 3 + 3]
                        cout = cin
                        cik, cok = "CC", "CC"
                    else:
                        o3 = ((l * 2 + si) * 8 + h) * 3
                        cin, cout = SCI[:, o3:o3 + 3], SCO[:, o3:o3 + 3]
                        cik, cok = "SCI", "SCO"
                    S.op("dve", lambda e, off=off, cin=cin: e.tensor_copy(out=XP[:, off:off + 3], in_=cin),
                         reads=[cik], writes=[kXP])
                    S.op("act", lambda e, off=off, c0=c0, sn=sn, pst=pst: e.activation(
                        out=XP[:, off + 3:off + 3 + sn], in_=pst[:, c0:c0 + sn], func=AF.Copy),
                        reads=[pkey], writes=[kXP])
                    w0 = pb + 32 + 0 * 8 + h
                    S.op("dve", lambda e, off=off, c0=c0, sn=sn, w0=w0, h=h: e.tensor_scalar(
                        out=XC[:, c0:c0 + sn], in0=XP[:, off:off + sn], scalar1=PAR[:, w0:w0 + 1],
                        scalar2=PAR[:, pb + 64 + h:pb + 65 + h], op0=ALU.mult, op1=ALU.add),
                        reads=[kXP, "PAR"], writes=[kXC])
                    for k in range(1, 4):
                        wk = pb + 32 + k * 8 + h
                        S.op("dve", lambda e, off=off, c0=c0, sn=sn, wk=wk, k=k: e.scalar_tensor_tensor(
                            out=XC[:, c0:c0 + sn], in0=XP[:, off + k:off + k + sn], scalar=PAR[:, wk:wk + 1],
                            in1=XC[:, c0:c0 + sn], op0=ALU.mult, op1=ALU.add),
                            reads=[kXP, kXC, "PAR"], writes=[kXC])
                    S.op("dve", lambda e, off=off, sn=sn, cout=cout: e.tensor_copy(
                        out=cout, in_=XP[:, off + sn:off + sn + 3]),
                        reads=[kXP], writes=[cok])
                S.op("act", lambda e: e.activation(out=XCB[:, :n], in_=XC[:, :n], func=AF.Copy),
                     reads=[kXC], writes=[kXCB])

            def stage2(h):
                bi = h % 2
                XC, XCB = XCs[bi], XCBs[bi]
                kXC, kXCB = f"XC{bi}", f"XCB{bi}"
                S.op("pe", lambda e: e.matmul(psB[0][:, :n], lhsT=WR[:, l, h, :], rhs=XCB[:, :n], start=True, stop=True),
                     reads=[kXCB, f"WR{l}"], writes=["psB0"])
                S.op("pe", lambda e: e.matmul(psB[1][:, :n], lhsT=WI[:, l, h, :], rhs=XCB[:, :n], start=True, stop=True),
                     reads=[kXCB, f"WI{l}"], writes=["psB1"])
                S.op("act", lambda e: e.activation(out=RR[:, :n], in_=psB[0][:, :n], func=AF.Sigmoid,
                                                   bias=PAR[:, pb + 72 + h:pb + 73 + h]),
                     reads=["psB0", "PAR"], writes=["RR"])
                S.op("act", lambda e: e.activation(out=II[:, :n], in_=psB[1][:, :n], func=AF.Sigmoid,
                                                   bias=PAR[:, pb + 80 + h:pb + 81 + h]),
                     reads=["psB1", "PAR"], writes=["II"])
                S.op("act", lambda e: e.activation(out=AA[:, :n], in_=RR[:, :n], func=AF.Exp,
                                                   scale=CEXP[:, l * 8 + h:l * 8 + h + 1]),
                     reads=["RR", "CEXP"], writes=["AA"])
                S.op("dve", lambda e: e.tensor_tensor(out=MM[:, :n], in0=AA[:, :n], in1=AA[:, :n], op=ALU.mult),
                     reads=["AA"], writes=["MM"])
                S.op("act", lambda e: e.activation(out=MM[:, :n], in_=MM[:, :n], func=AF.Sqrt, scale=-1.0, bias=1.0),
                     reads=["MM"], writes=["MM"])
                S.op("dve", lambda e: e.tensor_tensor(out=II[:, :n], in0=II[:, :n], in1=XC[:, :n], op=ALU.mult),
                     reads=["II", kXC], writes=["II"])
                S.op("dve", lambda e: e.tensor_tensor(out=II[:, :n], in0=II[:, :n], in1=MM[:, :n], op=ALU.mult),
                     reads=["II", "MM"], writes=["II"])
                for si, (c0, sn) in enumerate(segs):
                    if kind == "P":
                        hin = HC[:, l * 8 + h:l * 8 + h + 1]; hout = hin; hik = hok = "HC"
                    else:
                        o1 = (l * 2 + si) * 8 + h
                        hin, hout = SHI[:, o1:o1 + 1], SHO[:, o1:o1 + 1]; hik, hok = "SHI", "SHO"
                    S.op("dve", lambda e, c0=c0, sn=sn, hin=hin: e.tensor_tensor_scan(
                        out=RR[:, c0:c0 + sn], data0=AA[:, c0:c0 + sn], data1=II[:, c0:c0 + sn],
                        initial=hin, op0=ALU.mult, op1=ALU.add),
                        reads=["AA", "II", hik], writes=["RR"])
                    S.op("dve", lambda e, c0=c0, sn=sn, hout=hout: e.tensor_copy(
                        out=hout, in_=RR[:, c0 + sn - 1:c0 + sn]),
                        reads=["RR"], writes=[hok])
                if not full:
                    return
                S.op("dve", lambda e: e.tensor_tensor(out=OA[:, h, :n], in0=OA[:, h, :n], in1=RR[:, :n], op=ALU.mult),
                     reads=["RR", f"OA{h}"], writes=[f"OA{h}"])
                b = sqi[0] % 2
                sqi[0] += 1
                S.op("act", lambda e, b=b: e.activation(out=SQ[b][:, :n], in_=OA[:, h, :n], func=AF.Square),
                     reads=[f"OA{h}"], writes=[f"SQ{b}"])
                S.op("pe", lambda e, b=b: e.matmul(psS[:, :n], lhsT=ONES[:, :], rhs=SQ[b][:, :n],
                                                   start=(h == 0), stop=(h == 7)),
                     reads=[f"SQ{b}", "ONES"], writes=["psS"])

            def side(h):
                blk, hh = h // 2, h % 2
                wv, wkey = ga_w[blk]
                proj_chunk(psV[0], "psV0", wv, wkey, hh * 128, n)
                S.op("act", lambda e: e.activation(out=OA[:, h, :n], in_=psV[0][:, :n], func=AF.Gelu_apprx_tanh),
                     reads=["psV0"], writes=[f"OA{h}"])
                wv, wkey = u_w[blk]
                proj_chunk(psV[1], "psV1", wv, wkey, hh * 128, n)
                S.op("act", lambda e: e.activation(out=UB[:, h, :n], in_=psV[1][:, :n], func=AF.Gelu_apprx_tanh),
                     reads=["psV1"], writes=[f"UB{h}"])

            stage1(0)
            for h in range(8):
                if h + 1 < 8:
                    stage1(h + 1)
                if full:
                    side(h)
                stage2(h)
            if not full:
                return
            S.op("act", lambda e: e.activation(out=RSA[:, :n], in_=psS[:, :n], func=AF.Sqrt, scale=1.0 / GW, bias=EPS),
                 reads=["psS"], writes=["RSA"])
            S.op("dve", lambda e: e.reciprocal(out=RSA[:, :n], in_=RSA[:, :n]), reads=["RSA"], writes=["RSA"])
            vw = [load_w(wv_in[:, :, 3 * GW + blk * 256:3 * GW + (blk + 1) * 256], v16, f"wb_in{l}") for blk in range(4)]
            for tb in range(nb):
                for half in range(2):
                    for q in range(2):
                        wv, wkey = vw[half * 2 + q]
                        for kc in range(16):
                            S.op("pe", lambda e, kc=kc, wv=wv, half=half, q=q, tb=tb: e.matmul(
                                psV[half][:, q * 256:(q + 1) * 256], lhsT=XN[:, kc, tb * 128:(tb + 1) * 128],
                                rhs=wv[:, kc, :], start=(kc == 0), stop=(kc == 15)),
                                reads=[wkey, f"XN{kc}"], writes=[f"psV{half}"], signal=(kc == 15))
                    S.op("act", lambda e, half=half: e.activation(
                        out=VF[:, half * 512:(half + 1) * 512], in_=psV[half][:, :], func=AF.Gelu_apprx_tanh),
                        reads=[f"psV{half}"], writes=["VF"])
                    S.op("dve", lambda e, half=half: e.bn_stats(out=BST[:, half * 6:half * 6 + 6],
                                                                in_=VF[:, half * 512:(half + 1) * 512]),
                         reads=["VF"], writes=["BST"])
                S.op("dve", lambda e: e.bn_aggr(out=BST[:, 12:14], in_=BST[:, 0:12]), reads=["BST"], writes=["BST"])
                S.op("act", lambda e: e.activation(out=BST[:, 14:15], in_=BST[:, 13:14], func=AF.Sqrt, bias=EPS),
                     reads=["BST"], writes=["BST"])
                S.op("dve", lambda e: e.reciprocal(out=BST[:, 14:15], in_=BST[:, 14:15]), reads=["BST"], writes=["BST"])
                S.op("dve", lambda e: e.tensor_scalar(out=VF[:, :], in0=VF[:, :], scalar1=BST[:, 12:13],
                                                      scalar2=BST[:, 14:15], op0=ALU.subtract, op1=ALU.mult),
                     reads=["VF", "BST"], writes=["VF"])
                S.op("dve", lambda e: e.tensor_tensor(out=VF[:, :], in0=VF[:, :], in1=LNG[:, :], op=ALU.mult),
                     reads=["VF", "LNG"], writes=["VF"])
                S.op("dve", lambda e: e.tensor_tensor(out=VF[:, :], in0=VF[:, :], in1=LNB[:, :], op=ALU.add),
                     reads=["VF", "LNB"], writes=["VF"])
                S.op("act", lambda e, tb=tb: e.activation(out=VN[:, tb, :], in_=VF[:, :], func=AF.Copy),
                     reads=["VF"], writes=["VN"])
                if kind == "S":
                    S.op("sp", lambda e: e.dma_start(out=vrows[l], in_=VF[:, :]), reads=["VF"], writes=["vrows"],
                         dma_sem=v_sem)
            wmix = WST if kind == "P" else WSS
            for h in range(8):
                pst, pkey = nextA()
                for tb in range(nb):
                    S.op("pe", lambda e, h=h, tb=tb, pst=pst: e.matmul(
                        pst[:, tb * 128:(tb + 1) * 128], lhsT=VN[:, tb, h * 128:(h + 1) * 128], rhs=wmix[:, l, h, :],
                        start=True, stop=False),
                        reads=["VN", f"WST{l}", f"WSS{l}"], writes=[pkey], signal=False)
                    if kind == "P":
                        S.op("pe", lambda e, h=h, tb=tb, pst=pst: e.matmul(
                            pst[:, tb * 128:(tb + 1) * 128], lhsT=ONES[0:1, :],
                            rhs=BROW[0:1, l, h * 128:h * 128 + 128], start=False, stop=True),
                            reads=["BROW", "ONES"], writes=[pkey], signal=(tb == nb - 1))
                    else:
                        for half in range(2):
                            S.op("pe", lambda e, h=h, half=half, pst=pst: e.matmul(
                                pst[:, half * 64:half * 64 + 64], lhsT=ONES[0:1, :],
                                rhs=BROW[0:1, l, h * 128:h * 128 + 64], start=False, stop=True),
                                reads=["BROW", "ONES"], writes=[pkey], signal=(half == 1))
                S.op("dve", lambda e, h=h, pst=pst: e.tensor_tensor(out=UB[:, h, :n], in0=pst[:, :n], in1=UB[:, h, :n], op=ALU.mult),
                     reads=[pkey, f"UB{h}"], writes=[f"UB{h}"])
                b = sqi[0] % 2
                sqi[0] += 1
                S.op("act", lambda e, h=h, b=b: e.activation(out=SQ[b][:, :n], in_=UB[:, h, :n], func=AF.Square),
                     reads=[f"UB{h}"], writes=[f"SQ{b}"])
                S.op("pe", lambda e, h=h, b=b: e.matmul(psS[:, :n], lhsT=ONES[:, :], rhs=SQ[b][:, :n],
                                                        start=(h == 0), stop=(h == 7)),
                     reads=[f"SQ{b}", "ONES"], writes=["psS"])
            S.op("act", lambda e: e.activation(out=RSB[:, :n], in_=psS[:, :n], func=AF.Sqrt, scale=1.0 / GW, bias=EPS),
                 reads=["psS"], writes=["RSB"])
            S.op("dve", lambda e: e.reciprocal(out=RSB[:, :n], in_=RSB[:, :n]), reads=["RSB"], writes=["RSB"])
            for h in range(8):
                S.op("dve", lambda e, h=h: e.scalar_tensor_tensor(
                    out=XN[:, h, :n], in0=OA[:, h, :n], scalar=PAR[:, pb + 96 + h:pb + 97 + h], in1=RSA[:, :n],
                    op0=ALU.mult, op1=ALU.mult), reads=[f"OA{h}", "RSA", "PAR"], writes=[f"XN{h}"])
                S.op("dve", lambda e, h=h: e.scalar_tensor_tensor(
                    out=XN[:, 8 + h, :n], in0=UB[:, h, :n], scalar=PAR[:, pb + 104 + h:pb + 105 + h], in1=RSB[:, :n],
                    op0=ALU.mult, op1=ALU.mult), reads=[f"UB{h}", "RSB", "PAR"], writes=[f"XN{8 + h}"])
            wv_o = wb_out[l].rearrange("(k p) c -> p k c", p=128)
            for blk in range(8):
                wv, wkey = load_w(wv_o[:, :, blk * 256:(blk + 1) * 256], v16, f"wb_out{l}")
                for hh in range(2):
                    oc = blk * 2 + hh
                    pst, pkey = nextA()
                    proj_chunk(pst, pkey, wv, wkey, hh * 128, n)
                    S.op("dve", lambda e, oc=oc, pst=pst: e.tensor_tensor(out=X[:, oc, :n], in0=pst[:, :n], in1=X[:, oc, :n], op=ALU.add),
                         reads=[pkey, f"X{oc}"], writes=[f"X{oc}"])
            rms_stats(lambda kc: X[:, kc, :n], lambda kc: f"X{kc}", 16, n, D, RST, "RST")
            norm_to_bf16(n, pb + 16, RST, "RST")
            wv_g = wb_gate[l].rearrange("(k p) c -> p k c", p=128)
            wv_u = wb_up[l].rearrange("(k p) c -> p k c", p=128)
            wv_d = wb_down[l].rearrange("(k p) c -> p k c", p=128)
            NST = DFF // 256
            dslots = {}

            def ffn_gu(st):
                gv, gk = load_w(wv_g[:, :, st * 256:(st + 1) * 256], v16, f"wb_gate{l}")
                uv, uk = load_w(wv_u[:, :, st * 256:(st + 1) * 256], v16, f"wb_up{l}")
                dslots[st] = load_w(wv_d[:, st * 2:st * 2 + 2, :], v2, f"wb_down{l}")
                for j in range(2):
                    a = (st % 2) * 2 + j
                    proj_chunk(psA[j], f"psA{j}", gv, gk, j * 128, n)
                    proj_chunk(psB[j], f"psB{j}", uv, uk, j * 128, n)
                    S.op("act", lambda e, j=j: e.activation(out=GG[:, :n], in_=psA[j][:, :n], func=AF.Silu),
                         reads=[f"psA{j}"], writes=["MM"])
                    S.op("dve", lambda e, j=j, a=a: e.tensor_tensor(out=ACTB[:, a, :n], in0=psB[j][:, :n], in1=GG[:, :n], op=ALU.mult),
                         reads=[f"psB{j}", "MM"], writes=[f"ACTB{a}"])

            def ffn_d_pair(p):
                d0, k0 = dslots.pop(2 * p)
                d1, k1 = dslots.pop(2 * p + 1)
                srcs = [(d0, k0, 0, 0), (d0, k0, 1, 1), (d1, k1, 0, 2), (d1, k1, 1, 3)]
                for oc in range(16):
                    pv = psV[oc % 2]
                    pk = f"psV{oc % 2}"
                    for i, (dv, dk, j, a) in enumerate(srcs):
                        S.op("pe", lambda e, oc=oc, j=j, pv=pv, a=a, dv=dv, i=i: e.matmul(
                            pv[:, :n], lhsT=dv[:, j, oc * 128:(oc + 1) * 128], rhs=ACTB[:, a, :n],
                            start=(i == 0), stop=(i == 3)),
                            reads=[dk, f"ACTB{a}"], writes=[pk], signal=(i == 3))
                    S.op("dve", lambda e, oc=oc, pv=pv: e.tensor_tensor(out=X[:, oc, :n], in0=pv[:, :n], in1=X[:, oc, :n], op=ALU.add),
                         reads=[pk, f"X{oc}"], writes=[f"X{oc}"])

            assert NST % 2 == 0
            for st in range(NST):
                ffn_gu(st)
                if st % 2 == 1:
                    ffn_d_pair(st // 2)

        xTv = xT.rearrange("(k p) t -> p k t", p=128)
        yTv = yT.rearrange("(k p) t -> p k t", p=128)
        for ti, (col0, n, kind) in enumerate(tiles):
            for k0 in (0, 8):
                S.op("sp", lambda e, col0=col0, n=n, k0=k0: e.dma_start(
                    out=X[:, k0:k0 + 8, :n], in_=xTv[:, k0:k0 + 8, col0:col0 + n]),
                    writes=[f"X{kc}" for kc in range(k0, k0 + 8)], dma_sem=x_sems[k0])
            own = (kind == "S") or (ti >= NPRE)
            layer(0, n, kind, col0, "full")
            layer(1, n, kind, col0, "full" if own else "prefix")
            if kind == "P":
                S.op("dve", lambda e, ti=ti: e.tensor_scalar(
                    out=CC[:, :], in0=CC[:, :], scalar1=FLG[:, ti:ti + 1], scalar2=None, op0=ALU.mult),
                    reads=["CC", "FLG"], writes=["CC"])
                S.op("dve", lambda e, ti=ti: e.tensor_scalar(
                    out=HC[:, :], in0=HC[:, :], scalar1=FLG[:, ti:ti + 1], scalar2=None, op0=ALU.mult),
                    reads=["HC", "FLG"], writes=["HC"])
            if not own:
                continue
            ycol = (ti - NPRE) * T if kind == "P" else OWN * T
            rms_stats(lambda kc, n=n: X[:, kc, :n], lambda kc: f"X{kc}", 16, n, D, RST, "RST")
            for kc in range(16):
                S.op("dve", lambda e, kc=kc, n=n: e.scalar_tensor_tensor(
                    out=X[:, kc, :n], in0=X[:, kc, :n], scalar=PAR[:, 224 + kc:225 + kc], in1=RST[:, :n],
                    op0=ALU.mult, op1=ALU.mult), reads=[f"X{kc}", "RST", "PAR"], writes=[f"X{kc}"])
            for k0 in (0, 8):
                S.op("sp", lambda e, ycol=ycol, n=n, k0=k0: e.dma_start(
                    out=yTv[:, k0:k0 + 8, ycol:ycol + n], in_=X[:, k0:k0 + 8, :n]),
                    reads=[f"X{kc}" for kc in range(k0, k0 + 8)], writes=[f"yT{k0}"], dma_sem=y_sems[k0])
        S.op("sp", lambda e: e.dma_start(out=convP[:, :], in_=CC[:, :]), reads=["CC"], writes=["o1"], dma_sem=o_sem)
        S.op("sp", lambda e: e.dma_start(out=lruP[:, :], in_=HC[:, :]), reads=["HC"], writes=["o2"], dma_sem=o_sem)
        S.op("sp", lambda e: e.dma_start(out=convS[:, :], in_=SCO[:, :]), reads=["SCO"], writes=["o3"], dma_sem=o_sem)
        S.op("sp", lambda e: e.dma_start(out=lruS[:, :], in_=SHO[:, :]), reads=["SHO"], writes=["o4"], dma_sem=o_sem)
        n_y = (OWN + 1) * 16
        n_o = 4 * 16
        n_v = L * 16

        with nc.Block() as block:
            @block.sync
            def _(e):
                S.emit("sp", e)
                e.wait_ge(y_sems[0], n_y)
                e.wait_ge(y_sems[8], n_y)
                e.wait_ge(o_sem, n_o)
                e.wait_ge(v_sem, n_v)

            @block.gpsimd
            def _(e):
                S.emit("pool", e)

            @block.scalar
            def _(e):
                S.emit("act", e)

            @block.vector
            def _(e):
                S.emit("dve", e)

            @block.tensor
            def _(e):
                S.emit("pe", e)
    return nc


def _vec_pk(v, k):
    return np.ascontiguousarray(v.reshape(k, 128).T)


def run(NPT, x_prompt, x_sample, state_conv, state_lru, norm1, w_in, conv_w, conv_b,
        w_rgate, b_rgate, w_igate, b_igate, lru_param, v_ln_g, v_ln_b,
        w_spatial, b_spatial, gn_a, gn_b, w_out, norm2, w_gate, w_up, w_down, norm_f):
    f = lambda a: np.ascontiguousarray(np.asarray(a, dtype=np.float32))
    x_prompt, x_sample, state_conv, state_lru = map(f, (x_prompt, x_sample, state_conv, state_lru))
    B, SEQ, _ = x_prompt.shape
    assert SEQ == NPT * T and NPT % 4 == 0 and B == 2
    NTOK = SEQ + NS
    par = np.zeros((128, NPAR), np.float32)
    for l in range(L):
        pb = l * 112
        par[:, pb:pb + 16] = _vec_pk(f(norm1[l]), 16)
        par[:, pb + 16:pb + 32] = _vec_pk(f(norm2[l]), 16)
        for k in range(4):
            par[:, pb + 32 + k * 8:pb + 40 + k * 8] = _vec_pk(f(conv_w[l, k]), 8)
        par[:, pb + 64:pb + 72] = _vec_pk(f(conv_b[l]), 8)
        par[:, pb + 72:pb + 80] = _vec_pk(f(b_rgate[l]), 8)
        par[:, pb + 80:pb + 88] = _vec_pk(f(b_igate[l]), 8)
        par[:, pb + 88:pb + 96] = _vec_pk(f(lru_param[l]), 8)
        par[:, pb + 96:pb + 104] = _vec_pk(f(gn_a[l]), 8)
        par[:, pb + 104:pb + 112] = _vec_pk(f(gn_b[l]), 8)
    par[:, 224:240] = _vec_pk(f(norm_f), 16)
    wsT = np.ascontiguousarray(f(w_spatial).transpose(0, 3, 1, 2))
    bsp = np.ascontiguousarray(f(b_spatial).reshape(L, 1, GW))
    shared = dict(par=par, w_in=f(w_in), w_out=f(w_out), w_gate=f(w_gate), w_up=f(w_up), w_down=f(w_down),
                  w_rg=f(w_rgate), w_ig=f(w_igate), wsT=wsT, bsp=bsp, lng=f(v_ln_g), lnb=f(v_ln_b))
    SEG = SEQ // 4
    OWN = NPT // 4
    in_maps = []
    for c in range(8):
        b, q = c // 4, c % 4
        xT = np.zeros((D, NTOK), np.float32)
        nreal = (q + 1) * SEG
        xT[:, SEQ - nreal:SEQ] = x_prompt[b, :nreal].T
        flg = np.zeros((128, NPT), np.float32)
        flg[:, NPT - nreal // T:] = 1.0
        xs = x_sample[2 * c:2 * c + 2].reshape(NS, D)
        xT[:, SEQ:] = xs.T
        sc = state_conv[:, 2 * c:2 * c + 2]
        sci = sc.reshape(L, 2, 3, 8, 128).transpose(4, 0, 1, 3, 2).reshape(128, L * 2 * 8 * 3)
        sh = state_lru[:, 2 * c:2 * c + 2]
        shi = sh.reshape(L, 2, 8, 128).transpose(3, 0, 1, 2).reshape(128, L * 2 * 8)
        m = dict(shared)
        m.update(xT=xT, flg=flg, sci=np.ascontiguousarray(sci), shi=np.ascontiguousarray(shi))
        in_maps.append(m)
    nc = build_nc(NPT)
    res = run_bass_kernel_spmd(nc, in_maps, core_ids=list(range(8)))
    R = res.results
    y_prompt = np.stack([np.concatenate([R[4 * b + q]["yT"][:, :OWN * T].T for q in range(4)], axis=0)
                         for b in range(B)]).astype(np.float32)
    y_sample = np.concatenate([R[c]["yT"][:, OWN * T:].T.reshape(2, 64, D) for c in range(8)]).astype(np.float32)
    last = [4 * b + 3 for b in range(B)]
    ncp = np.stack([R[c]["convP"].reshape(128, L, 8, 3).transpose(1, 3, 2, 0).reshape(L, 3, GW) for c in last], axis=1)
    nlp = np.stack([R[c]["lruP"].reshape(128, L, 8).transpose(1, 2, 0).reshape(L, GW) for c in last], axis=1)
    ncs = np.concatenate([R[c]["convS"].reshape(128, L, 2, 8, 3).transpose(1, 2, 4, 3, 0).reshape(L, 2, 3, GW)
                          for c in range(8)], axis=1)
    nls = np.concatenate([R[c]["lruS"].reshape(128, L, 2, 8).transpose(1, 2, 3, 0).reshape(L, 2, GW)
                          for c in range(8)], axis=1)
    nvs = np.concatenate([R[c]["vrows"].reshape(L, 2, 64, GW) for c in range(8)], axis=1)
    out = (y_prompt, y_sample, ncp, nlp, ncs, nls, nvs)
    return tuple(np.ascontiguousarray(o, dtype=np.float32) for o in out)


def kernel(**inputs):
    return run(16, **inputs)
```

```python
import numpy as np
import concourse.bass as bass
import concourse.mybir as mybir
from concourse.bass_utils import run_bass_kernel_spmd

F32 = mybir.dt.float32
BF16 = mybir.dt.bfloat16
AF = mybir.ActivationFunctionType
ALU = mybir.AluOpType

D = 2048
GW = 1024
DFF = 5632
L = 2
T = 512
NS = 128
EPS = 1e-6
NPAR = 240
SLOT = 4096
NSLOT = 6


class Sched:
    def __init__(self, nc, esem):
        self.nc = nc
        self.esem = esem
        self.ops = {e: [] for e in esem}
        self.cnt = {e: 0 for e in esem}
        self.prog = {e: [] for e in esem}
        self.lastw = {}
        self.readers = {}
        self.waited = {e: {} for e in esem}
        self.dcnt = {}

    def _resolve(self, ref, eng):
        if ref[0] == "dma":
            return (ref[1], ref[2])
        e, idx = ref
        if e == eng and e == "pe":
            return None
        lst = self.ops[e]
        for j in range(idx, len(lst)):
            if lst[j] is not None:
                return (self.esem[e], lst[j])
        raise RuntimeError(f"unsignaled dep on {e} idx {idx}")

    def op(self, eng, fn, reads=(), writes=(), signal=True, dma_sem=None):
        deps = []
        for k in reads:
            if k in self.lastw:
                deps.append(self.lastw[k])
        for k in writes:
            if k in self.lastw:
                deps.append(self.lastw[k])
            deps.extend(self.readers.get(k, ()))
        waits = {}
        for d in deps:
            r = self._resolve(d, eng)
            if r is None:
                continue
            s, v = r
            key = id(s)
            if self.waited[eng].get(key, 0) >= v:
                continue
            if key not in waits or waits[key][1] < v:
                waits[key] = (s, v)
        for key, (s, v) in waits.items():
            self.waited[eng][key] = v
        if dma_sem is not None:
            self.dcnt[id(dma_sem)] = self.dcnt.get(id(dma_sem), 0) + 16
            ref = ("dma", dma_sem, self.dcnt[id(dma_sem)])
            self.ops[eng].append(None)
            inc = (dma_sem, 16)
        else:
            if signal:
                self.cnt[eng] += 1
                self.ops[eng].append(self.cnt[eng])
                inc = (self.esem[eng], 1)
            else:
                self.ops[eng].append(None)
                inc = None
            ref = (eng, len(self.ops[eng]) - 1)
        self.prog[eng].append((list(waits.values()), fn, inc))
        for k in writes:
            self.lastw[k] = ref
            self.readers[k] = []
        for k in reads:
            if k not in writes:
                self.readers.setdefault(k, []).append(ref)

    def emit(self, eng, e):
        for waits, fn, inc in self.prog[eng]:
            for s, v in waits:
                e.wait_ge(s, v)
            ins = fn(e)
            if inc is not None:
                ins.then_inc(inc[0], inc[1])


def build_nc(NPT):
    NTOK = NPT * T + NS
    nc = bass.Bass("TRN2", target_bir_lowering=False)
    dt_in = lambda name, shape: nc.dram_tensor(name, shape, F32, kind="ExternalInput").ap()
    dt_out = lambda name, shape: nc.dram_tensor(name, shape, F32, kind="ExternalOutput").ap()
    xT = dt_in("xT", [D, NTOK])
    par = dt_in("par", [128, NPAR])
    w_in = dt_in("w_in", [L, D, 4 * GW])
    w_out = dt_in("w_out", [L, D, D])
    w_gate = dt_in("w_gate", [L, D, DFF])
    w_up = dt_in("w_up", [L, D, DFF])
    w_down = dt_in("w_down", [L, DFF, D])
    w_rg = dt_in("w_rg", [L, 8, 128, 128])
    w_ig = dt_in("w_ig", [L, 8, 128, 128])
    wsT = dt_in("wsT", [L, 128, 8, 128])
    bsp = dt_in("bsp", [L, 1, GW])
    lng = dt_in("lng", [L, GW])
    lnb = dt_in("lnb", [L, GW])
    sci = dt_in("sci", [128, L * 2 * 8 * 3])
    shi = dt_in("shi", [128, L * 2 * 8])
    OWN = NPT // 4
    NPRE = NPT - OWN
    flg = dt_in("flg", [128, NPT])
    yT = dt_out("yT", [D, OWN * T + NS])
    convP = dt_out("convP", [128, L * 8 * 3])
    lruP = dt_out("lruP", [128, L * 8])
    convS = dt_out("convS", [128, L * 2 * 8 * 3])
    lruS = dt_out("lruS", [128, L * 2 * 8])
    vrows = dt_out("vrows", [L, NS, GW])

    tiles = [(i * T, T, "P") for i in range(NPT)] + [(NPT * T, NS, "S")]
    dt_scr = lambda name, shape: nc.dram_tensor(name, shape, BF16, kind="Internal").ap()
    wb_in = dt_scr("wb_in", [L, D, 4 * GW])
    wb_out = dt_scr("wb_out", [L, D, D])
    wb_gate = dt_scr("wb_gate", [L, D, DFF])
    wb_up = dt_scr("wb_up", [L, D, DFF])
    wb_down = dt_scr("wb_down", [L, DFF, D])

    import contextlib
    with contextlib.ExitStack() as es:
        sb = lambda name, shape, dt=F32: es.enter_context(nc.sbuf_tensor(name, shape, dt))
        ps = lambda name: es.enter_context(nc.psum_tensor(name, [128, 512], F32))
        sem = lambda name: es.enter_context(nc.semaphore(name))
        X = sb("X", [128, 16, T])
        XN = sb("XN", [128, 16, T], BF16)
        OA = sb("OA", [128, 8, T])
        UB = sb("UB", [128, 8, T])
        slots = [sb(f"slot{i}", [128, SLOT], BF16) for i in range(NSLOT)]
        PAR = sb("PAR", [128, NPAR])
        CEXP = sb("CEXP", [128, L * 8])
        TMP8 = sb("TMP8", [128, 4, L * 8])
        WR = sb("WR", [128, L, 8, 128], BF16)
        WI = sb("WI", [128, L, 8, 128], BF16)
        WST = sb("WST", [128, L, 8, 128], BF16)
        WSS = sb("WSS", [128, L, 8, 128], BF16)
        ONES = sb("ONES", [128, 128])
        BROW = sb("BROW", [1, L, GW])
        LNG = sb("LNG", [128, GW])
        LNB = sb("LNB", [128, GW])
        CC = sb("CC", [128, L * 8 * 3])
        HC = sb("HC", [128, L * 8])
        SCI = sb("SCI", [128, L * 2 * 8 * 3])
        SHI = sb("SHI", [128, L * 2 * 8])
        SCO = sb("SCO", [128, L * 2 * 8 * 3])
        SHO = sb("SHO", [128, L * 2 * 8])
        XPs = [sb(f"XP{i}", [128, T + 8]) for i in range(2)]
        XCs = [sb(f"XC{i}", [128, T]) for i in range(2)]
        XCBs = [sb(f"XCB{i}", [128, T], BF16) for i in range(2)]
        RR = sb("RR", [128, T])
        II = sb("II", [128, T])
        AA = sb("AA", [128, T])
        MM = sb("MM", [128, T])
        SQ = [sb(f"SQ{i}", [128, T]) for i in range(2)]
        RST = sb("RST", [128, T])
        RSA = sb("RSA", [128, T])
        RSB = sb("RSB", [128, T])
        VF = sb("VF", [128, GW])
        WSF = VF[:, :].rearrange("p (h t) -> p h t", h=8)
        GG = MM
        VN = sb("VN", [128, 4, GW], BF16)
        BST = sb("BST", [128, 16])
        ACTB = sb("ACTB", [128, 4, T], BF16)
        FLG = sb("FLG", [128, NPT])
        psA = [ps("psA0"), ps("psA1")]
        psB = [ps("psB0"), ps("psB1")]
        psS = ps("psS")
        psV = [ps("psV0"), ps("psV1")]
        esem = {e: sem("s_" + e) for e in ("pe", "act", "dve", "pool", "sp")}
        slot_sem = [sem(f"sl{i}") for i in range(NSLOT)]
        x_sems = {0: sem("xs0"), 8: sem("xs8")}
        y_sems = {0: sem("ys0"), 8: sem("ys8")}
        c_sem = sem("cs")
        ln_sem = sem("lns")
        o_sem = sem("os")
        v_sem = sem("vs")
        S = Sched(nc, esem)
        csems = []

        def c_new():
            csems.append(sem(f"c{len(csems)}"))
            return csems[-1]
        lng_sem = sem("lng_s")
        lnb_sem = sem("lnb_s")
        nslot = [0]

        def load_w(src_ap, view, skey):
            i = nslot[0] % NSLOT
            nslot[0] += 1
            key = f"slot{i}"
            dst = view(slots[i])
            nk = dst.shape[1]
            step = 8 if nk > 8 else nk
            for k0 in range(0, nk, step):
                S.op("sp", lambda e, d=dst, s=src_ap, k0=k0: e.dma_start(out=d[:, k0:k0 + step, :], in_=s[:, k0:k0 + step, :]),
                     reads=[skey], writes=[key], dma_sem=slot_sem[i])
            return dst, key

        def precast(dst, src, skey, rows, piece):
            sm = sem("pc_" + skey)
            for r0 in range(0, rows, piece):
                S.op("pool", lambda e, r0=r0: e.dma_start(out=dst[r0:r0 + piece, :], in_=src[r0:r0 + piece, :]),
                     writes=[skey], dma_sem=sm)

        v16 = lambda sl: sl[:, :].rearrange("p (k n) -> p k n", k=16)
        v2 = lambda sl: sl[:, :].rearrange("p (k n) -> p k n", k=2)

        S.op("sp", lambda e: e.dma_start(out=PAR[:, :], in_=par[:, :]), writes=["PAR"], dma_sem=c_new())
        S.op("sp", lambda e: e.dma_start(out=SCI[:, :], in_=sci[:, :]), writes=["SCI"], dma_sem=c_new())
        S.op("sp", lambda e: e.dma_start(out=SHI[:, :], in_=shi[:, :]), writes=["SHI"], dma_sem=c_new())
        S.op("sp", lambda e: e.dma_start(out=BROW[:, :, :], in_=bsp.rearrange("l o n -> o l n")),
             writes=["BROW"], dma_sem=c_new())
        S.op("sp", lambda e: e.dma_start(out=FLG[:, :], in_=flg[:, :]), writes=["FLG"], dma_sem=c_new())
        S.op("dve", lambda e: e.memset(ONES[:, :], 1.0), writes=["ONES"])
        S.op("dve", lambda e: e.memset(CC[:, :], 0.0), writes=["CC"])
        S.op("dve", lambda e: e.memset(HC[:, :], 0.0), writes=["HC"])
        S.op("dve", lambda e: e.memset(WSF[:, :, :], 0.0), writes=["VF", "VFb"])
        for l in range(L):
            S.op("pool", lambda e, l=l: e.dma_start(out=WR[:, l, :, :], in_=w_rg[l].rearrange("h i j -> i h j")),
                 writes=[f"WR{l}"], dma_sem=c_new())
            S.op("pool", lambda e, l=l: e.dma_start(out=WI[:, l, :, :], in_=w_ig[l].rearrange("h i j -> i h j")),
                 writes=[f"WI{l}"], dma_sem=c_new())
            S.op("pool", lambda e, l=l: e.dma_start(out=WST[:, l, :, :], in_=wsT[l]), writes=[f"WST{l}"], dma_sem=c_new())
            S.op("pool", lambda e, l=l: e.affine_select(
                out=WST[:, l, :, :], in_=WST[:, l, :, :], pattern=[[0, 8], [1, 128]],
                compare_op=ALU.is_ge, fill=0.0, base=0, channel_multiplier=-1),
                reads=[f"WST{l}"], writes=[f"WST{l}"])
            S.op("sp", lambda e, l=l: e.dma_start(out=WSF[0:64, :, 0:64], in_=wsT[l, 0:64, :, 0:64]),
                 writes=["VF"], dma_sem=c_new())
            S.op("sp", lambda e, l=l: e.dma_start(out=WSF[64:128, :, 64:128], in_=wsT[l, 0:64, :, 0:64]),
                 writes=["VFb"], dma_sem=c_new())
            S.op("pool", lambda e, l=l: e.affine_select(
                out=WSS[:, l, :, :], in_=WSF[:, :, :], pattern=[[0, 8], [1, 128]],
                compare_op=ALU.is_ge, fill=0.0, base=0, channel_multiplier=-1),
                reads=["VF", "VFb"], writes=[f"WSS{l}"])
        for l in range(L):
            lam = PAR[:, l * 112 + 88:l * 112 + 96]
            c = slice(l * 8, l * 8 + 8)
            S.op("dve", lambda e, lam=lam, c=c: e.tensor_scalar(
                out=TMP8[:, 0, c], in0=lam, scalar1=-1.0, scalar2=None, op0=ALU.mult),
                reads=["PAR"], writes=["T0"])
            S.op("act", lambda e, c=c: e.activation(out=TMP8[:, 1, c], in_=TMP8[:, 0, c], func=AF.Abs),
                 reads=["T0"], writes=["T1"])
            S.op("act", lambda e, c=c: e.activation(out=TMP8[:, 2, c], in_=TMP8[:, 1, c], func=AF.Exp, scale=-1.0),
                 reads=["T1"], writes=["T2"])
            S.op("act", lambda e, c=c: e.activation(out=TMP8[:, 3, c], in_=TMP8[:, 2, c], func=AF.Ln, bias=1.0),
                 reads=["T2"], writes=["T3"])
            S.op("dve", lambda e, c=c: e.tensor_scalar(
                out=TMP8[:, 0, c], in0=TMP8[:, 0, c], scalar1=0.0, scalar2=None, op0=ALU.max),
                reads=["T0"], writes=["T0"])
            S.op("dve", lambda e, c=c: e.tensor_tensor(out=TMP8[:, 0, c], in0=TMP8[:, 0, c], in1=TMP8[:, 3, c], op=ALU.add),
                 reads=["T0", "T3"], writes=["T0"])
            S.op("dve", lambda e, c=c: e.tensor_scalar(
                out=CEXP[:, c], in0=TMP8[:, 0, c], scalar1=-8.0, scalar2=None, op0=ALU.mult),
                reads=["T0"], writes=["CEXP"])

        for l in range(L):
            precast(wb_in[l], w_in[l], f"wb_in{l}", D, 512)
            precast(wb_out[l], w_out[l], f"wb_out{l}", D, 512)
            precast(wb_gate[l], w_gate[l], f"wb_gate{l}", D, 512)
            precast(wb_up[l], w_up[l], f"wb_up{l}", D, 512)
            precast(wb_down[l], w_down[l], f"wb_down{l}", DFF, 512)

        sqi = [0]

        def acc_sq(b, first, acc, acc_key, n):
            if first:
                S.op("pool", lambda e: e.tensor_copy(out=acc[:, :n], in_=SQ[b][:, :n]),
                     reads=[f"SQ{b}"], writes=[acc_key])
            else:
                S.op("pool", lambda e: e.tensor_tensor(out=acc[:, :n], in0=acc[:, :n], in1=SQ[b][:, :n], op=ALU.add),
                     reads=[f"SQ{b}", acc_key], writes=[acc_key])

        def finish_rstd(acc, acc_key, n, dim):
            S.op("pe", lambda e: e.matmul(psS[:, :n], lhsT=ONES[:, :], rhs=acc[:, :n], start=True, stop=True),
                 reads=[acc_key, "ONES"], writes=["psS"])
            S.op("act", lambda e: e.activation(out=acc[:, :n], in_=psS[:, :n], func=AF.Sqrt, scale=1.0 / dim, bias=EPS),
                 reads=["psS"], writes=[acc_key])
            S.op("dve", lambda e: e.reciprocal(out=acc[:, :n], in_=acc[:, :n]), reads=[acc_key], writes=[acc_key])

        def rms_stats(src_fn, keys, nchunks, n, dim, out_rstd, out_key):
            for kc in range(nchunks):
                b = sqi[0] % 2
                sqi[0] += 1
                S.op("act", lambda e, kc=kc, b=b: e.activation(out=SQ[b][:, :n], in_=src_fn(kc), func=AF.Square),
                     reads=[keys(kc)], writes=[f"SQ{b}"])
                acc_sq(b, kc == 0, out_rstd, out_key, n)
            finish_rstd(out_rstd, out_key, n, dim)

        def norm_to_bf16(n, gcol, rstd, rkey):
            for kc in range(16):
                S.op("dve", lambda e, kc=kc: e.scalar_tensor_tensor(
                    out=XN[:, kc, :n], in0=X[:, kc, :n], scalar=PAR[:, gcol + kc:gcol + kc + 1],
                    in1=rstd[:, :n], op0=ALU.mult, op1=ALU.mult),
                    reads=[f"X{kc}", rkey, "PAR"], writes=[f"XN{kc}"])

        def proj_chunk(pst, pkey, wv, wkey, c0, n):
            for kc in range(16):
                S.op("pe", lambda e, kc=kc: e.matmul(pst[:, :n], lhsT=wv[:, kc, c0:c0 + 128], rhs=XN[:, kc, :n],
                                                     start=(kc == 0), stop=(kc == 15)),
                     reads=[wkey, f"XN{kc}"], writes=[pkey], signal=(kc == 15))

        def layer(l, n, kind, col0, mode="full"):
            full = (mode == "full")
            pb = l * 112
            segs = [(0, n)] if kind == "P" else [(0, 64), (64, 64)]
            nb = n // 128
            if full:
                S.op("sp", lambda e: e.dma_start(out=LNG[:, :], in_=lng[l].partition_broadcast(128)),
                     writes=["LNG"], dma_sem=lng_sem)
                S.op("sp", lambda e: e.dma_start(out=LNB[:, :], in_=lnb[l].partition_broadcast(128)),
                     writes=["LNB"], dma_sem=lnb_sem)
            rms_stats(lambda kc: X[:, kc, :n], lambda kc: f"X{kc}", 16, n, D, RST, "RST")
            norm_to_bf16(n, pb + 0, RST, "RST")
            wv_in = wb_in[l].rearrange("(k p) c -> p k c", p=128)
            pi = [0]

            def nextA():
                pi[0] += 1
                return psA[pi[0] % 2], f"psA{pi[0] % 2}"
            xa_w = {}
            ga_w = {}
            u_w = {}

            def stage1(h):
                blk, hh = h // 2, h % 2
                if hh == 0:
                    xa_w[blk] = load_w(wv_in[:, :, blk * 256:(blk + 1) * 256], v16, f"wb_in{l}")
                    if full:
                        ga_w[blk] = load_w(wv_in[:, :, GW + blk * 256:GW + (blk + 1) * 256], v16, f"wb_in{l}")
                        u_w[blk] = load_w(wv_in[:, :, 2 * GW + blk * 256:2 * GW + (blk + 1) * 256], v16, f"wb_in{l}")
                wv, wkey = xa_w[blk]
                bi = h % 2
                XP, XC, XCB = XPs[bi], XCs[bi], XCBs[bi]
                kXP, kXC, kXCB = f"XP{bi}", f"XC{bi}", f"XCB{bi}"
                pst, pkey = nextA()
                proj_chunk(pst, pkey, wv, wkey, hh * 128, n)
                for si, (c0, sn) in enumerate(segs):
                    off = c0 + 3 * si
                    if kind == "P":
                        cin = CC[:, (l * 8 + h) * 3:(l * 8 + h) * 3 + 3]
                        cout = cin
                        cik, cok = "CC", "CC"
                    else:
                        o3 = ((l * 2 + si) * 8 + h) * 3
                        cin, cout = SCI[:, o3:o3 + 3], SCO[:, o3:o3 + 3]
                        cik, cok = "SCI", "SCO"
                    S.op("dve", lambda e, off=off, cin=cin: e.tensor_copy(out=XP[:, off:off + 3], in_=cin),
                         reads=[cik], writes=[kXP])
                    S.op("act", lambda e, off=off, c0=c0, sn=sn, pst=pst: e.activation(
                        out=XP[:, off + 3:off + 3 + sn], in_=pst[:, c0:c0 + sn], func=AF.Copy),
                        reads=[pkey], writes=[kXP])
                    w0 = pb + 32 + 0 * 8 + h
                    S.op("dve", lambda e, off=off, c0=c0, sn=sn, w0=w0, h=h: e.tensor_scalar(
                        out=XC[:, c0:c0 + sn], in0=XP[:, off:off + sn], scalar1=PAR[:, w0:w0 + 1],
                        scalar2=PAR[:, pb + 64 + h:pb + 65 + h], op0=ALU.mult, op1=ALU.add),
                        reads=[kXP, "PAR"], writes=[kXC])
                    for k in range(1, 4):
                        wk = pb + 32 + k * 8 + h
                        S.op("dve", lambda e, off=off, c0=c0, sn=sn, wk=wk, k=k: e.scalar_tensor_tensor(
                            out=XC[:, c0:c0 + sn], in0=XP[:, off + k:off + k + sn], scalar=PAR[:, wk:wk + 1],
                            in1=XC[:, c0:c0 + sn], op0=ALU.mult, op1=ALU.add),
                            reads=[kXP, kXC, "PAR"], writes=[kXC])
                    S.op("dve", lambda e, off=off, sn=sn, cout=cout: e.tensor_copy(
                        out=cout, in_=XP[:, off + sn:off + sn + 3]),
                        reads=[kXP], writes=[cok])
                S.op("act", lambda e: e.activation(out=XCB[:, :n], in_=XC[:, :n], func=AF.Copy),
                     reads=[kXC], writes=[kXCB])

            def stage2(h):
                bi = h % 2
                XC, XCB = XCs[bi], XCBs[bi]
                kXC, kXCB = f"XC{bi}", f"XCB{bi}"
                S.op("pe", lambda e: e.matmul(psB[0][:, :n], lhsT=WR[:, l, h, :], rhs=XCB[:, :n], start=True, stop=True),
                     reads=[kXCB, f"WR{l}"], writes=["psB0"])
                S.op("pe", lambda e: e.matmul(psB[1][:, :n], lhsT=WI[:, l, h, :], rhs=XCB[:, :n], start=True, stop=True),
                     reads=[kXCB, f"WI{l}"], writes=["psB1"])
                S.op("act", lambda e: e.activation(out=RR[:, :n], in_=psB[0][:, :n], func=AF.Sigmoid,
                                                   bias=PAR[:, pb + 72 + h:pb + 73 + h]),
                     reads=["psB0", "PAR"], writes=["RR"])
                S.op("act", lambda e: e.activation(out=II[:, :n], in_=psB[1][:, :n], func=AF.Sigmoid,
                                                   bias=PAR[:, pb + 80 + h:pb + 81 + h]),
                     reads=["psB1", "PAR"], writes=["II"])
                S.op("act", lambda e: e.activation(out=AA[:, :n], in_=RR[:, :n], func=AF.Exp,
                                                   scale=CEXP[:, l * 8 + h:l * 8 + h + 1]),
                     reads=["RR", "CEXP"], writes=["AA"])
                S.op("dve", lambda e: e.tensor_tensor(out=MM[:, :n], in0=AA[:, :n], in1=AA[:, :n], op=ALU.mult),
                     reads=["AA"], writes=["MM"])
                S.op("act", lambda e: e.activation(out=MM[:, :n], in_=MM[:, :n], func=AF.Sqrt, scale=-1.0, bias=1.0),
                     reads=["MM"], writes=["MM"])
                S.op("dve", lambda e: e.tensor_tensor(out=II[:, :n], in0=II[:, :n], in1=XC[:, :n], op=ALU.mult),
                     reads=["II", kXC], writes=["II"])
                S.op("dve", lambda e: e.tensor_tensor(out=II[:, :n], in0=II[:, :n], in1=MM[:, :n], op=ALU.mult),
                     reads=["II", "MM"], writes=["II"])
                for si, (c0, sn) in enumerate(segs):
                    if kind == "P":
                        hin = HC[:, l * 8 + h:l * 8 + h + 1]; hout = hin; hik = hok = "HC"
                    else:
                        o1 = (l * 2 + si) * 8 + h
                        hin, hout = SHI[:, o1:o1 + 1], SHO[:, o1:o1 + 1]; hik, hok = "SHI", "SHO"
                    S.op("dve", lambda e, c0=c0, sn=sn, hin=hin: e.tensor_tensor_scan(
                        out=RR[:, c0:c0 + sn], data0=AA[:, c0:c0 + sn], data1=II[:, c0:c0 + sn],
                        initial=hin, op0=ALU.mult, op1=ALU.add),
                        reads=["AA", "II", hik], writes=["RR"])
                    S.op("dve", lambda e, c0=c0, sn=sn, hout=hout: e.tensor_copy(
                        out=hout, in_=RR[:, c0 + sn - 1:c0 + sn]),
                        reads=["RR"], writes=[hok])
                if not full:
                    return
                S.op("dve", lambda e: e.tensor_tensor(out=OA[:, h, :n], in0=OA[:, h, :n], in1=RR[:, :n], op=ALU.mult),
                     reads=["RR", f"OA{h}"], writes=[f"OA{h}"])
                b = sqi[0] % 2
                sqi[0] += 1
                S.op("act", lambda e, b=b: e.activation(out=SQ[b][:, :n], in_=OA[:, h, :n], func=AF.Square),
                     reads=[f"OA{h}"], writes=[f"SQ{b}"])
                acc_sq(b, h == 0, RSA, "RSA", n)

            def side(h):
                blk, hh = h // 2, h % 2
                wv, wkey = ga_w[blk]
                proj_chunk(psV[0], "psV0", wv, wkey, hh * 128, n)
                S.op("act", lambda e: e.activation(out=OA[:, h, :n], in_=psV[0][:, :n], func=AF.Gelu_apprx_tanh),
                     reads=["psV0"], writes=[f"OA{h}"])
                wv, wkey = u_w[blk]
                proj_chunk(psV[1], "psV1", wv, wkey, hh * 128, n)
                S.op("act", lambda e: e.activation(out=UB[:, h, :n], in_=psV[1][:, :n], func=AF.Gelu_apprx_tanh),
                     reads=["psV1"], writes=[f"UB{h}"])

            stage1(0)
            for h in range(8):
                if h + 1 < 8:
                    stage1(h + 1)
                if full:
                    side(h)
                stage2(h)
            if not full:
                return
            finish_rstd(RSA, "RSA", n, GW)
            vw = [load_w(wv_in[:, :, 3 * GW + blk * 256:3 * GW + (blk + 1) * 256], v16, f"wb_in{l}") for blk in range(4)]
            for tb in range(nb):
                for half in range(2):
                    for q in range(2):
                        wv, wkey = vw[half * 2 + q]
                        for kc in range(16):
                            S.op("pe", lambda e, kc=kc, wv=wv, half=half, q=q, tb=tb: e.matmul(
                                psV[half][:, q * 256:(q + 1) * 256], lhsT=XN[:, kc, tb * 128:(tb + 1) * 128],
                                rhs=wv[:, kc, :], start=(kc == 0), stop=(kc == 15)),
                                reads=[wkey, f"XN{kc}"], writes=[f"psV{half}"], signal=(kc == 15))
                    S.op("act", lambda e, half=half: e.activation(
                        out=VF[:, half * 512:(half + 1) * 512], in_=psV[half][:, :], func=AF.Gelu_apprx_tanh),
                        reads=[f"psV{half}"], writes=["VF"])
                    S.op("dve", lambda e, half=half: e.bn_stats(out=BST[:, half * 6:half * 6 + 6],
                                                                in_=VF[:, half * 512:(half + 1) * 512]),
                         reads=["VF"], writes=["BST"])
                S.op("dve", lambda e: e.bn_aggr(out=BST[:, 12:14], in_=BST[:, 0:12]), reads=["BST"], writes=["BST"])
                S.op("act", lambda e: e.activation(out=BST[:, 14:15], in_=BST[:, 13:14], func=AF.Sqrt, bias=EPS),
                     reads=["BST"], writes=["BST"])
                S.op("dve", lambda e: e.reciprocal(out=BST[:, 14:15], in_=BST[:, 14:15]), reads=["BST"], writes=["BST"])
                S.op("dve", lambda e: e.tensor_scalar(out=VF[:, :], in0=VF[:, :], scalar1=BST[:, 12:13],
                                                      scalar2=BST[:, 14:15], op0=ALU.subtract, op1=ALU.mult),
                     reads=["VF", "BST"], writes=["VF"])
                S.op("dve", lambda e: e.tensor_tensor(out=VF[:, :], in0=VF[:, :], in1=LNG[:, :], op=ALU.mult),
                     reads=["VF", "LNG"], writes=["VF"])
                S.op("dve", lambda e: e.tensor_tensor(out=VF[:, :], in0=VF[:, :], in1=LNB[:, :], op=ALU.add),
                     reads=["VF", "LNB"], writes=["VF"])
                S.op("act", lambda e, tb=tb: e.activation(out=VN[:, tb, :], in_=VF[:, :], func=AF.Copy),
                     reads=["VF"], writes=["VN"])
                if kind == "S":
                    S.op("sp", lambda e: e.dma_start(out=vrows[l], in_=VF[:, :]), reads=["VF"], writes=["vrows"],
                         dma_sem=v_sem)
            wmix = WST if kind == "P" else WSS
            for h in range(8):
                pst, pkey = nextA()
                for tb in range(nb):
                    S.op("pe", lambda e, h=h, tb=tb, pst=pst: e.matmul(
                        pst[:, tb * 128:(tb + 1) * 128], lhsT=VN[:, tb, h * 128:(h + 1) * 128], rhs=wmix[:, l, h, :],
                        start=True, stop=False),
                        reads=["VN", f"WST{l}", f"WSS{l}"], writes=[pkey], signal=False)
                    if kind == "P":
                        S.op("pe", lambda e, h=h, tb=tb, pst=pst: e.matmul(
                            pst[:, tb * 128:(tb + 1) * 128], lhsT=ONES[0:1, :],
                            rhs=BROW[0:1, l, h * 128:h * 128 + 128], start=False, stop=True),
                            reads=["BROW", "ONES"], writes=[pkey], signal=(tb == nb - 1))
                    else:
                        for half in range(2):
                            S.op("pe", lambda e, h=h, half=half, pst=pst: e.matmul(
                                pst[:, half * 64:half * 64 + 64], lhsT=ONES[0:1, :],
                                rhs=BROW[0:1, l, h * 128:h * 128 + 64], start=False, stop=True),
                                reads=["BROW", "ONES"], writes=[pkey], signal=(half == 1))
                S.op("dve", lambda e, h=h, pst=pst: e.tensor_tensor(out=UB[:, h, :n], in0=pst[:, :n], in1=UB[:, h, :n], op=ALU.mult),
                     reads=[pkey, f"UB{h}"], writes=[f"UB{h}"])
                b = sqi[0] % 2
                sqi[0] += 1
                S.op("act", lambda e, h=h, b=b: e.activation(out=SQ[b][:, :n], in_=UB[:, h, :n], func=AF.Square),
                     reads=[f"UB{h}"], writes=[f"SQ{b}"])
                acc_sq(b, h == 0, RSB, "RSB", n)
            finish_rstd(RSB, "RSB", n, GW)
            for h in range(8):
                S.op("dve", lambda e, h=h: e.scalar_tensor_tensor(
                    out=XN[:, h, :n], in0=OA[:, h, :n], scalar=PAR[:, pb + 96 + h:pb + 97 + h], in1=RSA[:, :n],
                    op0=ALU.mult, op1=ALU.mult), reads=[f"OA{h}", "RSA", "PAR"], writes=[f"XN{h}"])
                S.op("dve", lambda e, h=h: e.scalar_tensor_tensor(
                    out=XN[:, 8 + h, :n], in0=UB[:, h, :n], scalar=PAR[:, pb + 104 + h:pb + 105 + h], in1=RSB[:, :n],
                    op0=ALU.mult, op1=ALU.mult), reads=[f"UB{h}", "RSB", "PAR"], writes=[f"XN{8 + h}"])
            wv_o = wb_out[l].rearrange("(k p) c -> p k c", p=128)
            for blk in range(8):
                wv, wkey = load_w(wv_o[:, :, blk * 256:(blk + 1) * 256], v16, f"wb_out{l}")
                for hh in range(2):
                    oc = blk * 2 + hh
                    pst, pkey = nextA()
                    proj_chunk(pst, pkey, wv, wkey, hh * 128, n)
                    S.op("dve", lambda e, oc=oc, pst=pst: e.tensor_tensor(out=X[:, oc, :n], in0=pst[:, :n], in1=X[:, oc, :n], op=ALU.add),
                         reads=[pkey, f"X{oc}"], writes=[f"X{oc}"])
            rms_stats(lambda kc: X[:, kc, :n], lambda kc: f"X{kc}", 16, n, D, RST, "RST")
            norm_to_bf16(n, pb + 16, RST, "RST")
            wv_g = wb_gate[l].rearrange("(k p) c -> p k c", p=128)
            wv_u = wb_up[l].rearrange("(k p) c -> p k c", p=128)
            wv_d = wb_down[l].rearrange("(k p) c -> p k c", p=128)
            NST = DFF // 256
            dslots = {}

            def ffn_gu(st):
                gv, gk = load_w(wv_g[:, :, st * 256:(st + 1) * 256], v16, f"wb_gate{l}")
                uv, uk = load_w(wv_u[:, :, st * 256:(st + 1) * 256], v16, f"wb_up{l}")
                dslots[st] = load_w(wv_d[:, st * 2:st * 2 + 2, :], v2, f"wb_down{l}")
                for j in range(2):
                    a = (st % 2) * 2 + j
                    proj_chunk(psA[j], f"psA{j}", gv, gk, j * 128, n)
                    proj_chunk(psB[j], f"psB{j}", uv, uk, j * 128, n)
                    S.op("act", lambda e, j=j: e.activation(out=GG[:, :n], in_=psA[j][:, :n], func=AF.Silu),
                         reads=[f"psA{j}"], writes=["MM"])
                    S.op("dve", lambda e, j=j, a=a: e.tensor_tensor(out=ACTB[:, a, :n], in0=psB[j][:, :n], in1=GG[:, :n], op=ALU.mult),
                         reads=[f"psB{j}", "MM"], writes=[f"ACTB{a}"])

            def ffn_d_pair(p):
                d0, k0 = dslots.pop(2 * p)
                d1, k1 = dslots.pop(2 * p + 1)
                srcs = [(d0, k0, 0, 0), (d0, k0, 1, 1), (d1, k1, 0, 2), (d1, k1, 1, 3)]
                for oc in range(16):
                    pv = psV[oc % 2]
                    pk = f"psV{oc % 2}"
                    for i, (dv, dk, j, a) in enumerate(srcs):
                        S.op("pe", lambda e, oc=oc, j=j, pv=pv, a=a, dv=dv, i=i: e.matmul(
                            pv[:, :n], lhsT=dv[:, j, oc * 128:(oc + 1) * 128], rhs=ACTB[:, a, :n],
                            start=(i == 0), stop=(i == 3)),
                            reads=[dk, f"ACTB{a}"], writes=[pk], signal=(i == 3))
                    S.op("dve", lambda e, oc=oc, pv=pv: e.tensor_tensor(out=X[:, oc, :n], in0=pv[:, :n], in1=X[:, oc, :n], op=ALU.add),
                         reads=[pk, f"X{oc}"], writes=[f"X{oc}"])

            assert NST % 2 == 0
            for st in range(NST):
                ffn_gu(st)
                if st % 2 == 1:
                    ffn_d_pair(st // 2)

        xTv = xT.rearrange("(k p) t -> p k t", p=128)
        yTv = yT.rearrange("(k p) t -> p k t", p=128)
        for ti, (col0, n, kind) in enumerate(tiles):
            for k0 in (0, 8):
                S.op("sp", lambda e, col0=col0, n=n, k0=k0: e.dma_start(
                    out=X[:, k0:k0 + 8, :n], in_=xTv[:, k0:k0 + 8, col0:col0 + n]),
                    writes=[f"X{kc}" for kc in range(k0, k0 + 8)], dma_sem=x_sems[k0])
            own = (kind == "S") or (ti >= NPRE)
            layer(0, n, kind, col0, "full")
            layer(1, n, kind, col0, "full" if own else "prefix")
            if kind == "P":
                S.op("dve", lambda e, ti=ti: e.tensor_scalar(
                    out=CC[:, :], in0=CC[:, :], scalar1=FLG[:, ti:ti + 1], scalar2=None, op0=ALU.mult),
                    reads=["CC", "FLG"], writes=["CC"])
                S.op("dve", lambda e, ti=ti: e.tensor_scalar(
                    out=HC[:, :], in0=HC[:, :], scalar1=FLG[:, ti:ti + 1], scalar2=None, op0=ALU.mult),
                    reads=["HC", "FLG"], writes=["HC"])
            if not own:
                continue
            ycol = (ti - NPRE) * T if kind == "P" else OWN * T
            rms_stats(lambda kc, n=n: X[:, kc, :n], lambda kc: f"X{kc}", 16, n, D, RST, "RST")
            for kc in range(16):
                S.op("dve", lambda e, kc=kc, n=n: e.scalar_tensor_tensor(
                    out=X[:, kc, :n], in0=X[:, kc, :n], scalar=PAR[:, 224 + kc:225 + kc], in1=RST[:, :n],
                    op0=ALU.mult, op1=ALU.mult), reads=[f"X{kc}", "RST", "PAR"], writes=[f"X{kc}"])
            for k0 in (0, 8):
                S.op("sp", lambda e, ycol=ycol, n=n, k0=k0: e.dma_start(
                    out=yTv[:, k0:k0 + 8, ycol:ycol + n], in_=X[:, k0:k0 + 8, :n]),
                    reads=[f"X{kc}" for kc in range(k0, k0 + 8)], writes=[f"yT{k0}"], dma_sem=y_sems[k0])
        S.op("sp", lambda e: e.dma_start(out=convP[:, :], in_=CC[:, :]), reads=["CC"], writes=["o1"], dma_sem=o_sem)
        S.op("sp", lambda e: e.dma_start(out=lruP[:, :], in_=HC[:, :]), reads=["HC"], writes=["o2"], dma_sem=o_sem)
        S.op("sp", lambda e: e.dma_start(out=convS[:, :], in_=SCO[:, :]), reads=["SCO"], writes=["o3"], dma_sem=o_sem)
        S.op("sp", lambda e: e.dma_start(out=lruS[:, :], in_=SHO[:, :]), reads=["SHO"], writes=["o4"], dma_sem=o_sem)
        n_y = (OWN + 1) * 16
        n_o = 4 * 16
        n_v = L * 16

        with nc.Block() as block:
            @block.sync
            def _(e):
                S.emit("sp", e)
                e.wait_ge(y_sems[0], n_y)
                e.wait_ge(y_sems[8], n_y)
                e.wait_ge(o_sem, n_o)
                e.wait_ge(v_sem, n_v)

            @block.gpsimd
            def _(e):
                S.emit("pool", e)

            @block.scalar
            def _(e):
                S.emit("act", e)

            @block.vector
            def _(e):
                S.emit("dve", e)

            @block.tensor
            def _(e):
                S.emit("pe", e)
    return nc


def _vec_pk(v, k):
    return np.ascontiguousarray(v.reshape(k, 128).T)


def run(NPT, x_prompt, x_sample, state_conv, state_lru, norm1, w_in, conv_w, conv_b,
        w_rgate, b_rgate, w_igate, b_igate, lru_param, v_ln_g, v_ln_b,
        w_spatial, b_spatial, gn_a, gn_b, w_out, norm2, w_gate, w_up, w_down, norm_f):
    f = lambda a: np.ascontiguousarray(np.asarray(a, dtype=np.float32))
    x_prompt, x_sample, state_conv, state_lru = map(f, (x_prompt, x_sample, state_conv, state_lru))
    B, SEQ, _ = x_prompt.shape
    assert SEQ == NPT * T and NPT % 4 == 0 and B == 2
    NTOK = SEQ + NS
    par = np.zeros((128, NPAR), np.float32)
    for l in range(L):
        pb = l * 112
        par[:, pb:pb + 16] = _vec_pk(f(norm1[l]), 16)
        par[:, pb + 16:pb + 32] = _vec_pk(f(norm2[l]), 16)
        for k in range(4):
            par[:, pb + 32 + k * 8:pb + 40 + k * 8] = _vec_pk(f(conv_w[l, k]), 8)
        par[:, pb + 64:pb + 72] = _vec_pk(f(conv_b[l]), 8)
        par[:, pb + 72:pb + 80] = _vec_pk(f(b_rgate[l]), 8)
        par[:, pb + 80:pb + 88] = _vec_pk(f(b_igate[l]), 8)
        par[:, pb + 88:pb + 96] = _vec_pk(f(lru_param[l]), 8)
        par[:, pb + 96:pb + 104] = _vec_pk(f(gn_a[l]), 8)
        par[:, pb + 104:pb + 112] = _vec_pk(f(gn_b[l]), 8)
    par[:, 224:240] = _vec_pk(f(norm_f), 16)
    wsT = np.ascontiguousarray(f(w_spatial).transpose(0, 3, 1, 2))
    bsp = np.ascontiguousarray(f(b_spatial).reshape(L, 1, GW))
    shared = dict(par=par, w_in=f(w_in), w_out=f(w_out), w_gate=f(w_gate), w_up=f(w_up), w_down=f(w_down),
                  w_rg=f(w_rgate), w_ig=f(w_igate), wsT=wsT, bsp=bsp, lng=f(v_ln_g), lnb=f(v_ln_b))
    SEG = SEQ // 4
    OWN = NPT // 4
    in_maps = []
    for c in range(8):
        b, q = c // 4, c % 4
        xT = np.zeros((D, NTOK), np.float32)
        nreal = (q + 1) * SEG
        xT[:, SEQ - nreal:SEQ] = x_prompt[b, :nreal].T
        flg = np.zeros((128, NPT), np.float32)
        flg[:, NPT - nreal // T:] = 1.0
        xs = x_sample[2 * c:2 * c + 2].reshape(NS, D)
        xT[:, SEQ:] = xs.T
        sc = state_conv[:, 2 * c:2 * c + 2]
        sci = sc.reshape(L, 2, 3, 8, 128).transpose(4, 0, 1, 3, 2).reshape(128, L * 2 * 8 * 3)
        sh = state_lru[:, 2 * c:2 * c + 2]
        shi = sh.reshape(L, 2, 8, 128).transpose(3, 0, 1, 2).reshape(128, L * 2 * 8)
        m = dict(shared)
        m.update(xT=xT, flg=flg, sci=np.ascontiguousarray(sci), shi=np.ascontiguousarray(shi))
        in_maps.append(m)
    nc = build_nc(NPT)
    res = run_bass_kernel_spmd(nc, in_maps, core_ids=list(range(8)))
    R = res.results
    y_prompt = np.stack([np.concatenate([R[4 * b + q]["yT"][:, :OWN * T].T for q in range(4)], axis=0)
                         for b in range(B)]).astype(np.float32)
    y_sample = np.concatenate([R[c]["yT"][:, OWN * T:].T.reshape(2, 64, D) for c in range(8)]).astype(np.float32)
    last = [4 * b + 3 for b in range(B)]
    ncp = np.stack([R[c]["convP"].reshape(128, L, 8, 3).transpose(1, 3, 2, 0).reshape(L, 3, GW) for c in last], axis=1)
    nlp = np.stack([R[c]["lruP"].reshape(128, L, 8).transpose(1, 2, 0).reshape(L, GW) for c in last], axis=1)
    ncs = np.concatenate([R[c]["convS"].reshape(128, L, 2, 8, 3).transpose(1, 2, 4, 3, 0).reshape(L, 2, 3, GW)
                          for c in range(8)], axis=1)
    nls = np.concatenate([R[c]["lruS"].reshape(128, L, 2, 8).transpose(1, 2, 3, 0).reshape(L, 2, GW)
                          for c in range(8)], axis=1)
    nvs = np.concatenate([R[c]["vrows"].reshape(L, 2, 64, GW) for c in range(8)], axis=1)
    out = (y_prompt, y_sample, ncp, nlp, ncs, nls, nvs)
    return tuple(np.ascontiguousarray(o, dtype=np.float32) for o in out)


def kernel(**inputs):
    return run(16, **inputs)
```
